# Optimizing a Trainium2 kernel written in Bass

```python
import math
import jax, jax.numpy as jnp
from jax import lax
import numpy as np

D_MODEL = 2048
BATCH = 8
SEQ = 4096
DEPTH = 1

CHUNK = 64
ATTN_WIDTH = D_MODEL // 2
SSM_WIDTH = D_MODEL - ATTN_WIDTH
HEAD_DIM = 64
N_HEADS = ATTN_WIDTH // HEAD_DIM
LEFT_CHUNKS = 8
BAND_CHUNKS = LEFT_CHUNKS + 1
REL_FUTURE = CHUNK - 1
REL_PAST = 128
N_REL = REL_FUTURE + REL_PAST + 1
NEG_INF = -1e30
SSM_GROUP = 16
N_SSM_GROUPS = SSM_WIDTH // SSM_GROUP
SSM_STATE = 64
DT_MIN = 1e-3
DT_MAX = 1e-1
D_FF = 256 * ((8 * D_MODEL // 3 + 255) // 256)
CONV_WIDTH = 3
LN_EPS = 1e-5
DEEPNORM_ALPHA = (2.0 * DEPTH) ** 0.25
DEEPNORM_BETA = (8.0 * DEPTH) ** -0.25

kernel_name = "hybrid_s5_chunkattn_deepnorm_encoder"


def layer_norm(x, g, b):
    xf = x.astype(jnp.float32)
    mu = jnp.mean(xf, axis=-1, keepdims=True)
    var = jnp.mean(jnp.square(xf - mu), axis=-1, keepdims=True)
    y = (xf - mu) * lax.rsqrt(var + LN_EPS) * g.astype(jnp.float32) + b.astype(jnp.float32)
    return y.astype(x.dtype)


def rms_norm(x, g):
    xf = x.astype(jnp.float32)
    ms = jnp.mean(jnp.square(xf), axis=-1, keepdims=True)
    return (xf * lax.rsqrt(ms + LN_EPS) * g.astype(jnp.float32)).astype(x.dtype)


def chunk_band_attention(q, k, v, rel_bias):
    bsz, seq, _ = q.shape
    nc = seq // CHUNK
    band = BAND_CHUNKS * CHUNK
    qc = q.reshape(bsz, nc, CHUNK, N_HEADS, HEAD_DIM)
    kc = k.reshape(bsz, nc, CHUNK, N_HEADS, HEAD_DIM)
    vc = v.reshape(bsz, nc, CHUNK, N_HEADS, HEAD_DIM)
    pad = ((0, 0), (LEFT_CHUNKS, 0), (0, 0), (0, 0), (0, 0))
    band_idx = np.arange(nc)[:, None] + np.arange(BAND_CHUNKS)[None, :]
    kb = jnp.pad(kc, pad)[:, band_idx].reshape(bsz, nc, band, N_HEADS, HEAD_DIM)
    vb = jnp.pad(vc, pad)[:, band_idx].reshape(bsz, nc, band, N_HEADS, HEAD_DIM)
    scores = jnp.einsum("bcqhd,bckhd->bhcqk", qc, kb).astype(jnp.float32) * (HEAD_DIM ** -0.5)
    rel = (LEFT_CHUNKS * CHUNK + np.arange(CHUNK)[:, None]) - np.arange(band)[None, :]
    rel_idx = np.clip(rel, -REL_FUTURE, REL_PAST) + REL_FUTURE
    bias = rel_bias.astype(jnp.float32)[:, rel_idx]
    valid = np.repeat(band_idx >= LEFT_CHUNKS, CHUNK, axis=1)
    scores = jnp.where(valid[None, None, :, None, :], scores + bias[None, :, None], NEG_INF)
    probs = jax.nn.softmax(scores, axis=-1).astype(v.dtype)
    out = jnp.einsum("bhcqk,bckhd->bcqhd", probs, vb)
    return out.reshape(bsz, seq, ATTN_WIDTH)


def _complex_linear_combine(e1, e2):
    a1r, a1i, b1r, b1i = e1
    a2r, a2i, b2r, b2i = e2
    return (a2r * a1r - a2i * a1i,
            a2r * a1i + a2i * a1r,
            a2r * b1r - a2i * b1i + b2r,
            a2r * b1i + a2i * b1r + b2i)


def s5_mixer(u, a_re, a_im, log_dt, b_re, b_im, c_re, c_im, d_skip):
    f32 = jnp.float32
    bsz, seq, _ = u.shape
    uf = u.astype(f32).reshape(bsz, seq, N_SSM_GROUPS, SSM_GROUP)
    a_re = a_re.astype(f32)
    a_im = a_im.astype(f32)
    dt = jnp.exp(log_dt.astype(f32))[:, None]
    decay = jnp.exp(dt * a_re)
    ab_re = decay * jnp.cos(dt * a_im)
    ab_im = decay * jnp.sin(dt * a_im)
    den = a_re * a_re + a_im * a_im
    zr = ab_re - 1.0
    f_re = (zr * a_re + ab_im * a_im) / den
    f_im = (ab_im * a_re - zr * a_im) / den
    b_re = b_re.astype(f32)
    b_im = b_im.astype(f32)
    bb_re = f_re[..., None] * b_re - f_im[..., None] * b_im
    bb_im = f_re[..., None] * b_im + f_im[..., None] * b_re
    bu_re = jnp.einsum("blgp,gnp->blgn", uf, bb_re)
    bu_im = jnp.einsum("blgp,gnp->blgn", uf, bb_im)
    shape = (1, seq, N_SSM_GROUPS, SSM_STATE)
    elems = (jnp.broadcast_to(ab_re, shape), jnp.broadcast_to(ab_im, shape), bu_re, bu_im)
    _, _, s_re, s_im = lax.associative_scan(_complex_linear_combine, elems, axis=1)
    y = (jnp.einsum("blgn,gpn->blgp", s_re, c_re.astype(f32))
         - jnp.einsum("blgn,gpn->blgp", s_im, c_im.astype(f32))
         + d_skip.astype(f32) * uf)
    return y.reshape(bsz, seq, SSM_WIDTH).astype(u.dtype)


def causal_depthwise_conv(u, w, b):
    out = lax.conv_general_dilated(
        u, w[:, None, :].astype(u.dtype), window_strides=(1,), padding=[(CONV_WIDTH - 1, 0)],
        dimension_numbers=("NWC", "WIO", "NWC"), feature_group_count=u.shape[-1])
    return out + b


def setup_inputs(seed: int = 0) -> dict:
    key = jax.random.key(seed)
    ks = jax.random.split(key, 24)
    f32 = jnp.float32
    L = DEPTH
    G, N, P = N_SSM_GROUPS, SSM_STATE, SSM_GROUP
    nrm = lambda k, s: jax.random.normal(k, s, f32)
    x = nrm(ks[0], (BATCH, SEQ, D_MODEL))
    w_in = nrm(ks[1], (L, D_MODEL, 3 * ATTN_WIDTH + SSM_WIDTH)) * D_MODEL ** -0.5
    attn_rel_bias = 0.1 * nrm(ks[2], (L, N_HEADS, N_REL))
    ssm_a_re = -0.5 + 0.01 * nrm(ks[3], (L, G, N))
    ssm_a_im = math.pi * jnp.arange(N, dtype=f32)[None, None, :] + 0.01 * nrm(ks[4], (L, G, N))
    ssm_log_dt = jax.random.uniform(ks[5], (L, G), f32, math.log(DT_MIN), math.log(DT_MAX))
    ssm_b_re = nrm(ks[6], (L, G, N, P)) * (2.0 * P) ** -0.5
    ssm_b_im = nrm(ks[7], (L, G, N, P)) * (2.0 * P) ** -0.5
    ssm_c_re = nrm(ks[8], (L, G, P, N)) * N ** -0.5
    ssm_c_im = nrm(ks[9], (L, G, P, N)) * N ** -0.5
    ssm_d = nrm(ks[10], (L, G, P))
    w_glu = nrm(ks[11], (L, SSM_WIDTH, SSM_WIDTH)) * SSM_WIDTH ** -0.5
    b_glu = 0.01 * nrm(ks[12], (L, SSM_WIDTH))
    g_attn_out = 1.0 + 0.01 * nrm(ks[13], (L, ATTN_WIDTH))
    g_ssm_out = 1.0 + 0.01 * nrm(ks[14], (L, SSM_WIDTH))
    w_out = nrm(ks[15], (L, D_MODEL, D_MODEL)) * D_MODEL ** -0.5 * DEEPNORM_BETA
    ln1_g = 1.0 + 0.01 * nrm(ks[16], (L, D_MODEL))
    ln1_b = 0.01 * nrm(ks[17], (L, D_MODEL))
    w_ffn_in = nrm(ks[18], (L, D_MODEL, 2 * D_FF)) * D_MODEL ** -0.5
    ffn_conv_w = nrm(ks[19], (L, CONV_WIDTH, D_FF)) * CONV_WIDTH ** -0.5
    ffn_conv_b = 0.01 * nrm(ks[20], (L, D_FF))
    w_ffn_out = nrm(ks[21], (L, D_FF, D_MODEL)) * D_FF ** -0.5 * DEEPNORM_BETA
    ln2_g = 1.0 + 0.01 * nrm(ks[22], (L, D_MODEL))
    ln2_b = 0.01 * nrm(ks[23], (L, D_MODEL))
    return {"x": x, "w_in": w_in, "attn_rel_bias": attn_rel_bias,
            "ssm_a_re": ssm_a_re, "ssm_a_im": ssm_a_im, "ssm_log_dt": ssm_log_dt,
            "ssm_b_re": ssm_b_re, "ssm_b_im": ssm_b_im, "ssm_c_re": ssm_c_re, "ssm_c_im": ssm_c_im,
            "ssm_d": ssm_d, "w_glu": w_glu, "b_glu": b_glu,
            "g_attn_out": g_attn_out, "g_ssm_out": g_ssm_out, "w_out": w_out,
            "ln1_g": ln1_g, "ln1_b": ln1_b, "w_ffn_in": w_ffn_in,
            "ffn_conv_w": ffn_conv_w, "ffn_conv_b": ffn_conv_b, "w_ffn_out": w_ffn_out,
            "ln2_g": ln2_g, "ln2_b": ln2_b}


def reference(x, w_in, attn_rel_bias, ssm_a_re, ssm_a_im, ssm_log_dt, ssm_b_re, ssm_b_im,
              ssm_c_re, ssm_c_im, ssm_d, w_glu, b_glu, g_attn_out, g_ssm_out, w_out,
              ln1_g, ln1_b, w_ffn_in, ffn_conv_w, ffn_conv_b, w_ffn_out, ln2_g, ln2_b):
    for l in range(DEPTH):
        proj = x @ w_in[l]
        q = proj[..., :ATTN_WIDTH]
        k = proj[..., ATTN_WIDTH:2 * ATTN_WIDTH]
        v = proj[..., 2 * ATTN_WIDTH:3 * ATTN_WIDTH]
        u = proj[..., 3 * ATTN_WIDTH:]
        attn = chunk_band_attention(q, k, v, attn_rel_bias[l])
        ssm = s5_mixer(u, ssm_a_re[l], ssm_a_im[l], ssm_log_dt[l], ssm_b_re[l], ssm_b_im[l],
                       ssm_c_re[l], ssm_c_im[l], ssm_d[l])
        ssm = jax.nn.gelu(ssm)
        ssm = ssm * jax.nn.sigmoid(ssm @ w_glu[l] + b_glu[l])
        mixed = jnp.concatenate([rms_norm(attn, g_attn_out[l]), rms_norm(ssm, g_ssm_out[l])], axis=-1)
        mixed = mixed @ w_out[l]
        x = layer_norm(DEEPNORM_ALPHA * x + mixed, ln1_g[l], ln1_b[l])
        up = x @ w_ffn_in[l]
        gate = causal_depthwise_conv(up[..., :D_FF], ffn_conv_w[l], ffn_conv_b[l])
        hidden = jax.nn.gelu(gate) * up[..., D_FF:]
        ff = hidden @ w_ffn_out[l]
        x = layer_norm(DEEPNORM_ALPHA * x + ff, ln2_g[l], ln2_b[l])
    return x
```

```python
import math
from contextlib import ExitStack
import numpy as np
import concourse.bass as bass
import concourse.mybir as mybir
from concourse.bass_utils import run_bass_kernel_spmd

F32 = mybir.dt.float32
BF16 = mybir.dt.bfloat16
AF = mybir.ActivationFunctionType
ALU = mybir.AluOpType

D = 2048
AW = 1024
NH = 16
DFF = 5632
TT = 512
ALPHA = 2.0 ** 0.25
EPS = 1e-5
MAGIC = 12582912.0
NEGM = -30000.0


class Res:
    __slots__ = ("name", "w", "r", "sem", "semcnt")

    def __init__(self, name=""):
        self.name = name
        self.w = None
        self.r = []
        self.sem = None
        self.semcnt = 0


class Eng:
    ROT = 6000

    def __init__(self, fw, e, name):
        self.fw = fw
        self.e = e
        self.name = name
        self.sem = None
        self.count = 0
        self.seen = {}
        self.pending = []
        self.nsem = 0
        self.ninst = 0
        self.allsems = []

    def _newsem(self):
        self.sem = self.fw.new_sem(f"{self.name}{self.nsem}")
        self.nsem += 1
        self.count = 0
        self.allsems.append(self.sem)

    def _need(self, waits, t):
        if t is None:
            return
        s, v = t[0], t[1]
        if self.seen.get(id(s), 0) >= v:
            return
        k = id(s)
        if k not in waits or waits[k][1] < v:
            waits[k] = (s, v)

    def wait_for(self, reads=(), writes=()):
        waits = {}
        for r in reads:
            self._need(waits, r.w)
        for w in writes:
            self._need(waits, w.w)
            for t in w.r:
                if t[2] is not self:
                    self._need(waits, t)
        for s, v in waits.values():
            self.e.wait_ge(s, v)
            self.seen[id(s)] = v

    def op(self, fn, reads=(), writes=(), inc=True):
        self.wait_for(reads, writes)
        ins = fn()
        self.ninst += 1
        if inc:
            if self.sem is None or self.count >= self.ROT:
                self._newsem()
            self.count += 1
            ins.then_inc(self.sem, 1)
            t = (self.sem, self.count, self)
            for r in self.pending:
                r.r.append(t)
            self.pending = []
            for r in reads:
                r.r.append(t)
            for w in writes:
                w.w = t
                w.r = []
        else:
            assert not writes
            self.pending.extend(reads)
        return ins

    def dma(self, out, in_, owner, reads=(), writes=(), **kw):
        self.wait_for(reads, writes)
        if owner.sem is None:
            owner.sem = self.fw.new_sem("d" + owner.name)
            owner.semcnt = 0
            self.fw.dma_owners.append(owner)
        owner.semcnt += 16
        ins = self.e.dma_start(out=out, in_=in_, **kw)
        ins.then_inc(owner.sem, 16)
        self.ninst += 1
        t = (owner.sem, owner.semcnt, None)
        for r in reads:
            r.r.append(t)
        for w in writes:
            w.w = t
            w.r = []
        return t


class FW:
    def __init__(self, nc, es):
        self.nc = nc
        self.es = es
        self.nsem = 0
        self.dma_owners = []
        self.allres = []
        self.pe = Eng(self, nc.tensor, "pe")
        self.act = Eng(self, nc.scalar, "act")
        self.dve = Eng(self, nc.vector, "dve")
        self.pool = Eng(self, nc.gpsimd, "pool")
        self.sp = Eng(self, nc.sync, "sp")
        self.engs = [self.pe, self.act, self.dve, self.pool, self.sp]

    def new_sem(self, name):
        self.nsem += 1
        return self.es.enter_context(self.nc.semaphore(f"s{self.nsem}_{name}"))

    def res(self, name):
        r = Res(name)
        self.allres.append(r)
        return r

    def sb(self, name, shape, dt):
        return self.es.enter_context(self.nc.sbuf_tensor(name, shape, dt))

    def ps(self, name, shape, dt):
        return self.es.enter_context(self.nc.psum_tensor(name, shape, dt))

    def barrier(self, engines=None):
        for e in self.engs:
            assert not e.pending, e.name
        ticks = []
        for e in self.engs:
            if e.sem is not None and e.count > 0:
                ticks.append((e.sem, e.count, e))
        for o in self.dma_owners:
            ticks.append((o.sem, o.semcnt, None))
        for e in (engines or self.engs):
            for s, v, src in ticks:
                if src is e:
                    continue
                if e.seen.get(id(s), 0) >= v:
                    continue
                e.e.wait_ge(s, v)
                e.seen[id(s)] = v
        if engines is None:
            for r in self.allres:
                r.w = None
                r.r = []


def build(L, dbg=False):
    NT = L // TT
    nc = bass.Bass("TRN2", target_bir_lowering=False)

    def din(name, shape, dt=F32):
        return nc.dram_tensor(name, list(shape), dt, kind="ExternalInput").ap()

    def dscr(name, shape, dt):
        kind = "ExternalOutput" if dbg else "Internal"
        return nc.dram_tensor(name, list(shape), dt, kind=kind).ap()

    x_d = din("x", [L, D])
    w_in_b = din("w_in_b", [8, 128, 8192])
    w_glu_b = din("w_glu_b", [2, 128, 4096])
    w_out_b = din("w_out_b", [4, 128, 8192])
    w_ffi_b = din("w_ffi_b", [22, 128, 8192])
    w_ffo_b = din("w_ffo_b", [16, 128, 5632])
    ident_d = din("ident", [128, 128])
    cmask_d = din("cmask", [128, 128])
    mask0_d = din("mask0", [128, 128])
    mask4_d = din("mask4", [128, 128])
    bias0_d = din("bias0", [NH, 128, 128])
    bias1_d = din("bias1", [NH, 128, 128])
    cfar_d = din("cfar", [128, NH])
    sare_d = din("s_are", [128, 32])
    saim_d = din("s_aim", [128, 32])
    sldt_d = din("s_ldt", [128, 32])
    sbre_d = din("s_bre", [128, 512])
    sbim_d = din("s_bim", [128, 512])
    scre_d = din("s_cre", [128, 512])
    scim_d = din("s_cim", [128, 512])
    sdcol_d = din("s_dcol", [128, 64])
    bglu_d = din("b_glu", [128, 8])
    gatt_d = din("g_att", [128, 8])
    gssm_d = din("g_ssm", [128, 8])
    ln1g_d = din("ln1_g", [D])
    ln1b_d = din("ln1_b", [D])
    ln2g_d = din("ln2_g", [D])
    ln2b_d = din("ln2_b", [D])
    convw_d = din("conv_w", [128, 44 * 3])
    convb_d = din("conv_b", [128, 44])
    out_d = nc.dram_tensor("out", [L, D], F32, kind="ExternalOutput").ap()

    qT_d = dscr("qT_s", [AW, L], BF16)
    kT_d = dscr("kT_s", [AW, L], BF16)
    v_d = dscr("v_s", [L, AW], BF16)
    u_d = dscr("u_s", [NT, 8, AW, 64], BF16)
    y_d = dscr("y_s", [NT, 8, AW, 64], BF16)
    at_d = dscr("at_s", [AW, L], BF16)
    x1_d = dscr("x1_s", [L, D], F32)

    es = ExitStack()
    with es:
        fw = FW(nc, es)
        pe, act, dve, pool, sp = fw.pe, fw.act, fw.dve, fw.pool, fw.sp
        V, A, P, G = nc.vector, nc.scalar, nc.tensor, nc.gpsimd

        ARENA_F32 = 40192
        arena = fw.sb("arena", [128, ARENA_F32], F32)
        apos = [0]

        def carve(shape, dt):
            n = 1
            for s_ in shape[1:]:
                n *= s_
            nf = n if dt == F32 else (n + 1) // 2
            nf = (nf + 7) // 8 * 8
            a0 = apos[0]
            assert a0 + nf <= ARENA_F32, ("arena overflow", a0, nf)
            apos[0] += nf
            v = arena[:, a0:a0 + nf]
            if dt == BF16:
                v = v.bitcast(BF16)
            v = v[:, 0:n]
            if len(shape) > 2:
                names = " ".join(f"d{i}" for i in range(1, len(shape)))
                kw = {f"d{i}": shape[i] for i in range(1, len(shape))}
                v = v.rearrange(f"p ({names}) -> p {names}", **kw)
            return v[0:shape[0]]

        def arena_reset():
            apos[0] = 0

        ident = fw.sb("identb", [128, 128], BF16)
        identf = fw.sb("identf", [128, 128], F32)
        onesb = fw.sb("onesb", [128, 128], BF16)
        R_const = fw.res("const")
        WR = [fw.sb(f"wr{i}", [128, 8192], BF16) for i in range(3)]
        RW = [fw.res(f"wr{i}") for i in range(3)]
        PB = [fw.ps(f"pb{i}", [128, 512], F32) for i in range(6)]
        RPB = [fw.res(f"pb{i}") for i in range(6)]
        PT = [fw.ps(f"pt{i}", [128, 1024], BF16) for i in range(2)]
        RPT = [fw.res(f"pt{i}") for i in range(2)]
        pbrot = [0]

        def next_pb(lo=0, hi=6):
            i = lo + pbrot[0] % (hi - lo)
            pbrot[0] += 1
            return PB[i], RPB[i]

        pool.dma(ident[:], ident_d[:, :], R_const, writes=[R_const])
        sp.dma(identf[:], ident_d[:, :], R_const, writes=[R_const])
        dve.op(lambda: V.memset(onesb[:], 1.0), writes=[R_const])

        class Ring:
            def __init__(self):
                self.seq = []
                self.issued = 0
                self.used = 0

            def start(self, seq):
                self.seq = seq
                self.issued = 0
                self.used = 0
                self.base = getattr(self, "base", 0)

            def prefetch(self, upto):
                while self.issued < min(upto, len(self.seq)):
                    src, n = self.seq[self.issued]
                    slot = (self.base + self.issued) % 3
                    pool.dma(WR[slot][:, 0:n], src, RW[slot], writes=[RW[slot]])
                    self.issued += 1

            def get(self, ahead=3):
                self.prefetch(self.used + ahead)
                slot = (self.base + self.used) % 3
                self.used += 1
                return WR[slot], RW[slot]

            def end(self):
                assert self.used == len(self.seq)
                self.base = (self.base + self.used) % 3

        ring = Ring()

        def mm_group(ps_ap, rps, pairs, reads):
            pe.wait_for(reads=reads, writes=[rps])
            n = len(pairs)
            for i, (l, r) in enumerate(pairs):
                last = (i == n - 1)
                pe.op(lambda: P.matmul(ps_ap, lhsT=l, rhs=r, start=(i == 0), stop=last),
                      reads=reads, writes=([rps] if last else []), inc=last)

        evrot = [0]

        def evac(out, in_, reads, writes, scale=None, eng=None):
            if eng is None:
                eng = "act" if (evrot[0] % 2 == 0) else "dve"
                evrot[0] += 1
            if eng == "act":
                if scale is None:
                    act.op(lambda: A.copy(out=out, in_=in_), reads=reads, writes=writes)
                else:
                    act.op(lambda: A.mul(out=out, in_=in_, mul=scale), reads=reads, writes=writes)
            else:
                if scale is None:
                    dve.op(lambda: V.tensor_copy(out=out, in_=in_), reads=reads, writes=writes)
                else:
                    dve.op(lambda: V.tensor_scalar_mul(out=out, in0=in_, scalar1=scale), reads=reads, writes=writes)

        def load_T(src_d, t0, XB, RXB, xT, RxT, eng=None):
            for s in range(4):
                xb = XB[s % 2]
                rxb = RXB[s % 2]
                pool.dma(xb[:], src_d[t0 + 128 * s: t0 + 128 * (s + 1), :], rxb, writes=[rxb])
                for half in range(2):
                    pt = PT[half]
                    pe.wait_for(reads=[rxb, R_const], writes=[RPT[half]])
                    for k in range(8):
                        kc = half * 8 + k
                        pe.op(lambda: P.transpose(pt[:, k * 128:(k + 1) * 128], xb[:, kc * 128:(kc + 1) * 128], ident[:]),
                              reads=[rxb, R_const], writes=([RPT[half]] if k == 7 else []), inc=(k == 7))
                    evac(xT[:, half * 8:half * 8 + 8, s * 128:(s + 1) * 128],
                         pt[:].rearrange("p (k c) -> p k c", k=8), [RPT[half]], [RxT], eng=eng)

        arena_reset()
        WZ = carve([128, 64, 128], BF16)
        RY = carve([128, 64, 128], BF16)
        KK = carve([128, 64, 128], BF16)
        AR2 = carve([128, 32, 2], F32)
        AIS = carve([128, 32, 2], F32)
        R_der = fw.res("derived")
        mark = apos[0]
        Rt = fw.res("dtmp")

        def small():
            return carve([128, 32], F32)

        are, aim, ldt = small(), small(), small()
        bre, bim, cre, cim = (carve([128, 32, 16], F32) for _ in range(4))
        dcol = carve([128, 64], F32)
        cmask = carve([128, 128], F32)
        halfpi = carve([128, 1], F32)
        for dst, src in ((are, sare_d), (aim, saim_d), (ldt, sldt_d), (dcol, sdcol_d), (cmask, cmask_d)):
            sp.dma(dst, src[:, :], Rt, writes=[Rt])
        for dst, src in ((bre, sbre_d), (bim, sbim_d), (cre, scre_d), (cim, scim_d)):
            sp.dma(dst, src[:, :].rearrange("p (g c) -> p g c", c=16), Rt, writes=[Rt])

        def dv(fn):
            dve.op(fn, reads=[Rt], writes=[Rt])

        def ac(fn):
            act.op(fn, reads=[Rt], writes=[Rt])

        dv(lambda: V.memset(halfpi, math.pi / 2))
        dt_, rho, th, E1, Einv, kk, thr, absx, s1, c1 = (small() for _ in range(10))
        ar1, ai1, arI, aiI, zr, den, f_re, f_im, t1, t2 = (small() for _ in range(10))
        ac(lambda: A.activation(out=dt_, in_=ldt, func=AF.Exp))
        dv(lambda: V.tensor_tensor(out=rho, in0=dt_, in1=are, op=ALU.mult))
        dv(lambda: V.tensor_tensor(out=th, in0=dt_, in1=aim, op=ALU.mult))
        ac(lambda: A.activation(out=E1, in_=rho, func=AF.Exp))
        ac(lambda: A.activation(out=Einv, in_=rho, func=AF.Exp, scale=-1.0))
        dv(lambda: V.tensor_scalar(out=kk, in0=th, scalar1=1.0 / (2 * math.pi), scalar2=MAGIC, op0=ALU.mult, op1=ALU.add))
        dv(lambda: V.tensor_scalar(out=kk, in0=kk, scalar1=MAGIC, scalar2=None, op0=ALU.subtract))
        dv(lambda: V.scalar_tensor_tensor(out=thr, in0=kk, scalar=-2 * math.pi, in1=th, op0=ALU.mult, op1=ALU.add))
        ac(lambda: A.activation(out=s1, in_=thr, func=AF.Sin))
        ac(lambda: A.activation(out=absx, in_=thr, func=AF.Abs))
        ac(lambda: A.activation(out=c1, in_=absx, func=AF.Sin, scale=-1.0, bias=halfpi))
        dv(lambda: V.tensor_tensor(out=ar1, in0=E1, in1=c1, op=ALU.mult))
        dv(lambda: V.tensor_tensor(out=ai1, in0=E1, in1=s1, op=ALU.mult))
        dv(lambda: V.tensor_tensor(out=arI, in0=Einv, in1=c1, op=ALU.mult))
        dv(lambda: V.scalar_tensor_tensor(out=aiI, in0=Einv, scalar=-1.0, in1=s1, op0=ALU.mult, op1=ALU.mult))
        dv(lambda: V.tensor_scalar(out=zr, in0=ar1, scalar1=-1.0, scalar2=None, op0=ALU.add))
        dv(lambda: V.tensor_tensor(out=den, in0=are, in1=are, op=ALU.mult))
        dv(lambda: V.tensor_tensor(out=t1, in0=aim, in1=aim, op=ALU.mult))
        dv(lambda: V.tensor_tensor(out=den, in0=den, in1=t1, op=ALU.add))
        dv(lambda: V.reciprocal(out=den, in_=den))
        dv(lambda: V.tensor_tensor(out=t1, in0=zr, in1=are, op=ALU.mult))
        dv(lambda: V.tensor_tensor(out=t2, in0=ai1, in1=aim, op=ALU.mult))
        dv(lambda: V.tensor_tensor(out=t1, in0=t1, in1=t2, op=ALU.add))
        dv(lambda: V.tensor_tensor(out=f_re, in0=t1, in1=den, op=ALU.mult))
        dv(lambda: V.tensor_tensor(out=t1, in0=ai1, in1=are, op=ALU.mult))
        dv(lambda: V.tensor_tensor(out=t2, in0=zr, in1=aim, op=ALU.mult))
        dv(lambda: V.tensor_tensor(out=t1, in0=t1, in1=t2, op=ALU.subtract))
        dv(lambda: V.tensor_tensor(out=f_im, in0=t1, in1=den, op=ALU.mult))

        def bc16(a):
            return a.unsqueeze(2).to_broadcast([128, 32, 16])

        def cmul(o_re, o_im, a_re, a_im, b_re, b_im, ta, tb, neg_im=False):
            dv(lambda: V.tensor_tensor(out=ta, in0=a_re, in1=b_re, op=ALU.mult))
            dv(lambda: V.tensor_tensor(out=tb, in0=a_im, in1=b_im, op=ALU.mult))
            dv(lambda: V.tensor_tensor(out=o_re, in0=ta, in1=tb, op=ALU.subtract))
            dv(lambda: V.tensor_tensor(out=ta, in0=a_re, in1=b_im, op=ALU.mult))
            dv(lambda: V.tensor_tensor(out=tb, in0=a_im, in1=b_re, op=ALU.mult))
            if neg_im:
                dv(lambda: V.scalar_tensor_tensor(out=o_im, in0=ta, scalar=-1.0, in1=tb, op0=ALU.mult, op1=ALU.subtract))
            else:
                dv(lambda: V.tensor_tensor(out=o_im, in0=ta, in1=tb, op=ALU.add))

        bbr = carve([128, 32, 16], F32)
        bbi = carve([128, 32, 16], F32)
        ta16 = carve([128, 32, 16], F32)
        tb16 = carve([128, 32, 16], F32)
        cmul(bbr, bbi, bc16(f_re), bc16(f_im), bre, bim, ta16, tb16)
        PWr = carve([128, 9, 32], F32)
        PWi = carve([128, 9, 32], F32)
        PIr = carve([128, 9, 32], F32)
        PIi = carve([128, 9, 32], F32)
        dv(lambda: V.tensor_copy(out=PWr[:, 1, :], in_=ar1))
        dv(lambda: V.tensor_copy(out=PWi[:, 1, :], in_=ai1))
        dv(lambda: V.tensor_copy(out=PIr[:, 1, :], in_=arI))
        dv(lambda: V.tensor_copy(out=PIi[:, 1, :], in_=aiI))
        for k in range(1, 8):
            cmul(PWr[:, k + 1, :], PWi[:, k + 1, :], PWr[:, k, :], PWi[:, k, :], ar1, ai1, t1, t2)
            cmul(PIr[:, k + 1, :], PIi[:, k + 1, :], PIr[:, k, :], PIi[:, k, :], arI, aiI, t1, t2)
        for ri in range(2):
            dve.op(lambda: V.tensor_copy(out=AR2[:, :, ri], in_=PWr[:, 8, :]), reads=[Rt], writes=[Rt, R_der])
        dve.op(lambda: V.tensor_scalar(out=AIS[:, :, 0], in0=PWi[:, 8, :], scalar1=-1.0, scalar2=None, op0=ALU.mult), reads=[Rt], writes=[Rt, R_der])
        dve.op(lambda: V.tensor_copy(out=AIS[:, :, 1], in_=PWi[:, 8, :]), reads=[Rt], writes=[Rt, R_der])
        Qr = carve([128, 32, 128], F32)
        Qi = carve([128, 32, 128], F32)
        Rr = carve([128, 32, 128], F32)
        Ri = carve([128, 32, 128], F32)
        for j in range(8):
            sl = slice(j * 16, (j + 1) * 16)
            cmul(Qr[:, :, sl], Qi[:, :, sl], bc16(PIr[:, j + 1, :]), bc16(PIi[:, j + 1, :]), bbr, bbi, ta16, tb16)
            cmul(Rr[:, :, sl], Ri[:, :, sl], cre, cim, bc16(PWr[:, j + 1, :]), bc16(PWi[:, j + 1, :]), ta16, tb16, neg_im=True)
        for gh in range(2):
            rows = slice(gh * 64, gh * 64 + 64)
            dve.op(lambda: V.tensor_copy(out=RY[0:64, gh * 32:(gh + 1) * 32, :], in_=Rr[rows, :, :]), reads=[Rt], writes=[R_der])
            dve.op(lambda: V.tensor_copy(out=RY[64:128, gh * 32:(gh + 1) * 32, :], in_=Ri[rows, :, :]), reads=[Rt], writes=[R_der])
        ktmp = carve([128, 4, 128], F32)
        Rk = fw.res("ktmp")
        for gb in range(16):
            gh = gb // 8
            rows = slice(gh * 64, gh * 64 + 64)
            pbk, rpbk = next_pb()
            pbw, rpbw = next_pb()
            pe.wait_for(reads=[Rt, R_const], writes=[rpbk, rpbw])
            for gi in range(4):
                gl = (gb % 8) * 4 + gi
                cs = slice(gi * 128, (gi + 1) * 128)
                pe.op(lambda: P.matmul(pbk[:, cs], lhsT=Qr[rows, gl, :], rhs=Rr[rows, gl, :], start=True, stop=False), reads=[Rt], inc=False)
                pe.op(lambda: P.matmul(pbk[:, cs], lhsT=Qi[rows, gl, :], rhs=Ri[rows, gl, :], start=False, stop=True),
                      reads=[Rt], writes=([rpbk] if gi == 3 else []), inc=(gi == 3))
            for gi in range(4):
                gl = (gb % 8) * 4 + gi
                pe.op(lambda: P.matmul(pbw[:, gi * 128:gi * 128 + 64], lhsT=Qr[rows, gl, :], rhs=identf[rows, gh * 64:gh * 64 + 64], start=True, stop=True),
                      reads=[Rt, R_const], inc=False)
                pe.op(lambda: P.matmul(pbw[:, gi * 128 + 64:gi * 128 + 128], lhsT=Qi[rows, gl, :], rhs=identf[rows, gh * 64:gh * 64 + 64], start=True, stop=True),
                      reads=[Rt, R_const], writes=([rpbw] if gi == 3 else []), inc=(gi == 3))
            g0 = gb * 4
            dve.op(lambda: V.tensor_tensor(out=ktmp[:, :, :], in0=pbk[:, :].rearrange("p (a b) -> p a b", a=4),
                                           in1=cmask.unsqueeze(1).to_broadcast([128, 4, 128]), op=ALU.mult),
                   reads=[rpbk, Rt], writes=[Rk])
            for gi in range(4):
                dve.op(lambda: V.scalar_tensor_tensor(out=KK[:, g0 + gi, :], in0=identf[:, :], scalar=dcol[:, g0 + gi:g0 + gi + 1],
                                                      in1=ktmp[:, gi, :], op0=ALU.mult, op1=ALU.add),
                       reads=[Rk, Rt, R_const], writes=[R_der])
            act.op(lambda: A.copy(out=WZ[:, g0:g0 + 4, :], in_=pbw[:, :].rearrange("p (a b) -> p a b", a=4)), reads=[rpbw], writes=[R_der])
        fw.barrier()
        apos[0] = mark
        XB = [carve([128, D], BF16) for _ in range(1)] * 2
        RXB = [fw.res("xb0")] * 2
        XT = [carve([128, 16, TT], BF16) for _ in range(2)]
        RXT = [fw.res(f"xT{i}") for i in range(2)]
        STG = [carve([128, TT], BF16) for _ in range(3)]
        RSTG = [fw.res(f"stg{i}") for i in range(3)]
        stgrot = [0]

        def next_stg():
            i = stgrot[0] % 3
            stgrot[0] += 1
            return STG[i], RSTG[i]

        UG = [carve([128, 64, 64], BF16) for _ in range(2)]
        RUG = [fw.res(f"ug{i}") for i in range(2)]
        ZSB = carve([128, 32, 2, 64], F32)
        RZ = fw.res("zsb")
        SH = carve([128, 65, 32, 2], F32)
        RSH = fw.res("sh")
        RSHh = [fw.res("sh0"), fw.res("sh1")]
        SQ2 = carve([128, 64, 64], BF16)
        RSQ2 = fw.res("sq2")
        YSB = [carve([128, 64, 64], BF16) for _ in range(1)]
        RYSB = [fw.res(f"ysb{i}") for i in range(1)]
        TS = carve([128, 32, 2], F32)
        P1 = carve([128, 32, 2], F32)
        P2 = carve([128, 32, 2], F32)
        dve.op(lambda: V.memset(SH[:, 0, :, :], 0.0), writes=[RSH, RSHh[0], RSHh[1]])
        R_UD = [[fw.res(f"ud{t}_{k}") for k in range(8)] for t in range(NT)]
        ring.start([(w_in_b[b], 8192) for _ in range(NT) for b in range(8)])

        def proj_tile(T, hook=None):
            t0 = T * TT
            xT = XT[T % 2]
            RxT = RXT[T % 2]
            if T == 0:
                load_T(x_d, t0, XB, RXB, xT, RxT, eng="act")
            for b in range(8):
                if b == 2 and hook is not None:
                    hook()
                if b == 6 and T + 1 < NT:
                    load_T(x_d, t0 + TT, XB, RXB, XT[(T + 1) % 2], RXT[(T + 1) % 2], eng="act")
                wt, rw = ring.get()
                w3 = wt[:, :].rearrange("p (k c) -> p k c", k=16)
                if b < 6:
                    for ft in range(4):
                        pb, rpb = next_pb()
                        mm_group(pb[:, :], rpb,
                                 [(w3[:, kc, ft * 128:(ft + 1) * 128], xT[:, kc, :]) for kc in range(16)],
                                 [rw, RxT])
                        st, rst = next_stg()
                        f0 = (b % 2) * 512 + ft * 128
                        if b < 2:
                            evac(st[:, :], pb[:, :], [rpb], [rst], scale=0.125, eng="act")
                            sp.dma(qT_d[f0:f0 + 128, t0:t0 + TT], st[:, :], rst, reads=[rst])
                        elif b < 4:
                            evac(st[:, :], pb[:, :], [rpb], [rst], eng="act")
                            sp.dma(kT_d[f0:f0 + 128, t0:t0 + TT], st[:, :], rst, reads=[rst])
                        else:
                            evac(st[:, :].rearrange("p (j m) -> p j m", j=8),
                                 pb[:, :].rearrange("p (m j) -> p j m", j=8), [rpb], [rst], eng="act")
                            sp.dma(u_d[T, :, f0:f0 + 128, :].rearrange("j q m -> q j m"),
                                   st[:, :].rearrange("p (j m) -> p j m", j=8), rst, reads=[rst], writes=[R_UD[T][(b - 4) * 4 + ft]])
                else:
                    for s in range(4):
                        pb, rpb = next_pb()
                        mm_group(pb[:, :], rpb,
                                 [(xT[:, kc, s * 128:(s + 1) * 128], w3[:, kc, :]) for kc in range(16)],
                                 [rw, RxT])
                        st, rst = next_stg()
                        evac(st[:, :], pb[:, :], [rpb], [rst], eng="act")
                        c0 = (b - 6) * 512
                        sp.dma(v_d[t0 + s * 128:t0 + (s + 1) * 128, c0:c0 + 512], st[:, :], rst, reads=[rst])

        def load_ug(T):
            ug, rug = UG[T % 2], RUG[T % 2]
            for j in range(8):
                sp.dma(ug[16 * j:16 * (j + 1), :, :], u_d[T, j].rearrange("(g p) m -> p g m", p=16), rug, reads=R_UD[T], writes=[rug])

        def ssm_front(T):
            ug, rug = UG[T % 2], RUG[T % 2]
            for gb in range(16):
                pb, rpb = next_pb()
                pe.wait_for(reads=[rug, R_der], writes=[rpb])
                for gi in range(4):
                    g = gb * 4 + gi
                    for ri in range(2):
                        last = (gi == 3 and ri == 1)
                        c0 = (gi * 2 + ri) * 64
                        pe.op(lambda: P.matmul(pb[0:64, c0:c0 + 64], lhsT=WZ[:, g, ri * 64:(ri + 1) * 64], rhs=ug[:, g, :], start=True, stop=True),
                              reads=[rug, R_der], writes=([rpb] if last else []), inc=last)
                gh = gb // 8
                gl0 = (gb % 8) * 4
                evac(ZSB[gh * 64:gh * 64 + 64, gl0:gl0 + 4, :, :],
                     pb[0:64, :].rearrange("p (a b c) -> p a b c", a=4, b=2), [rpb], [RZ], eng="act")
            halves = ((RSHh[0], slice(0, 64)), (RSHh[1], slice(64, 128)))
            for m in range(64):
                for (rs_, hs) in halves:
                    dve.op(lambda: V.tensor_tensor(out=TS[hs, :, :], in0=SH[hs, m, :, :], in1=ZSB[hs, :, :, m], op=ALU.add), reads=[rs_, RZ], writes=[rs_])
                for (rs_, hs) in halves:
                    dve.op(lambda: V.tensor_tensor(out=P1[hs, :, :], in0=TS[hs, :, :], in1=AR2[hs, :, :], op=ALU.mult), reads=[rs_, R_der], writes=[rs_])
                for (rs_, hs) in halves:
                    dve.op(lambda: V.tensor_tensor(out=P2[hs, :, :], in0=TS[hs, :, ::-1], in1=AIS[hs, :, :], op=ALU.mult), reads=[rs_, R_der], writes=[rs_])
                for (rs_, hs) in halves:
                    dve.op(lambda: V.tensor_tensor(out=SH[hs, m + 1, :, :], in0=P1[hs, :, :], in1=P2[hs, :, :], op=ALU.add), reads=[rs_], writes=[rs_])

        def ssm_back_a(T):
            ug, rug = UG[T % 2], RUG[T % 2]
            for gh in range(2):
                for ri in range(2):
                    act.op(lambda: A.copy(out=SQ2[ri * 64:(ri + 1) * 64, gh * 32:(gh + 1) * 32, :],
                                          in_=SH[gh * 64:(gh + 1) * 64, 0:64, :, ri].rearrange("p m g -> p g m")),
                           reads=[RSH, RSHh[0], RSHh[1]], writes=[RSQ2])
            dve.op(lambda: V.tensor_copy(out=SH[:, 0, :, :], in_=SH[:, 64, :, :]), reads=[RSH, RSQ2, RSHh[0], RSHh[1]], writes=[RSH, RSHh[0], RSHh[1]])

        def ssm_back_b(T):
            ug, rug = UG[T % 2], RUG[T % 2]
            ysb, rysb = YSB[0], RYSB[0]
            for gb in range(8):
                pb, rpb = next_pb()
                pe.wait_for(reads=[rug, R_der, RSQ2], writes=[rpb])
                for gi in range(8):
                    g = gb * 8 + gi
                    cs = slice(gi * 64, (gi + 1) * 64)
                    pe.op(lambda: P.matmul(pb[:, cs], lhsT=RY[:, g, :], rhs=SQ2[:, g, :], start=True, stop=False), reads=[R_der, RSQ2], inc=False)
                    pe.op(lambda: P.matmul(pb[:, cs], lhsT=KK[:, g, :], rhs=ug[:, g, :], start=False, stop=True),
                          reads=[R_der, rug], writes=([rpb] if gi == 7 else []), inc=(gi == 7))
                evac(ysb[:, gb * 8:(gb + 1) * 8, :], pb[:, :].rearrange("p (a b) -> p a b", a=8), [rpb], [rysb], eng="act")
            for j in range(8):
                sp.dma(y_d[T, j].rearrange("(g q) m -> q g m", q=16), ysb[16 * j:16 * (j + 1), :, :], rysb, reads=[rysb])


        for T in range(NT + 2):
            if 1 <= T <= NT:
                ssm_front(T - 1)
            hook = (lambda t=T: ssm_back_b(t - 2)) if T >= 2 else None
            if T < NT:
                proj_tile(T, hook=hook)
                load_ug(T)
            elif hook is not None:
                hook()
            if 1 <= T <= NT:
                ssm_back_a(T - 1)
        ring.end()
        fw.barrier()

        arena_reset()
        BT0 = carve([128, NH, 128], BF16)
        BT1 = carve([128, NH, 128], BF16)
        M4 = carve([128, 128], BF16)
        CF = carve([128, NH], F32)
        R_bt = fw.res("bt")
        mark = apos[0]
        btmp = carve([128, NH, 128], F32)
        mtmp = carve([128, 128], F32)
        Rb = fw.res("btmp")
        sp.dma(btmp, bias0_d.rearrange("h k q -> k h q"), Rb, writes=[Rb])
        sp.dma(mtmp, mask0_d[:, :], Rb, writes=[Rb])
        sp.dma(CF, cfar_d[:, :], R_bt, writes=[R_bt])
        dve.op(lambda: V.tensor_tensor(out=BT0[:, :, :], in0=btmp[:, :, :], in1=mtmp.unsqueeze(1).to_broadcast([128, NH, 128]), op=ALU.add),
               reads=[Rb], writes=[R_bt])
        sp.dma(btmp, bias1_d.rearrange("h k q -> k h q"), Rb, writes=[Rb])
        sp.dma(mtmp, mask4_d[:, :], Rb, writes=[Rb])
        dve.op(lambda: V.tensor_copy(out=BT1[:, :, :], in_=btmp[:, :, :]), reads=[Rb], writes=[R_bt])
        dve.op(lambda: V.tensor_copy(out=M4[:, :], in_=mtmp[:, :]), reads=[Rb], writes=[R_bt])
        fw.barrier()
        apos[0] = mark
        QT = [carve([128, 8, TT], BF16) for _ in range(2)]
        RQT = [fw.res(f"qt{i}") for i in range(2)]
        KT = [carve([128, 8, 2 * TT], BF16) for _ in range(2)]
        RKT = [fw.res(f"kt{i}") for i in range(2)]
        VA = [carve([128, 8, NH, 128], BF16) for _ in range(2)]
        RVA = [fw.res(f"va{i}") for i in range(2)]
        EE = [carve([128, 5, 128], BF16) for _ in range(2)]
        REE = [fw.res(f"ee{i}") for i in range(2)]
        AST = [carve([128, 8, TT], BF16) for _ in range(2)]
        RAST = [fw.res(f"ast{i}") for i in range(2)]
        RD = [carve([128, TT], F32) for _ in range(2)]
        RRD = [fw.res(f"rd{i}") for i in range(2)]
        for i in range(2):
            dve.op(lambda: V.memset(VA[i][:, :, :, 64:128], 1.0), writes=[RVA[i]])
        ecnt = 0
        for T in range(NT):
            t0 = T * TT
            qt, rqt = QT[T % 2], RQT[T % 2]
            kt, rkt = KT[T % 2], RKT[T % 2]
            va, rva = VA[T % 2], RVA[T % 2]
            ast, rast = AST[T % 2], RAST[T % 2]
            sp.dma(qt, qT_d[:, t0:t0 + TT].rearrange("(kc p) t -> p kc t", p=128), rqt, writes=[rqt])
            if T == 0:
                sp.dma(kt[:, :, TT:2 * TT], kT_d[:, 0:TT].rearrange("(kc p) t -> p kc t", p=128), rkt, writes=[rkt])
            else:
                sp.dma(kt, kT_d[:, t0 - TT:t0 + TT].rearrange("(kc p) t -> p kc t", p=128), rkt, writes=[rkt])
            for sl in range(8):
                k0 = t0 - TT + sl * 128
                if k0 < 0:
                    continue
                sp.dma(va[:, sl, :, 0:64], v_d[k0:k0 + 128, :].rearrange("t (h d) -> t h d", h=NH), rva, writes=[rva])
            items = [(h, i) for h in range(NH) for i in range(4)]

            def emit_qk(n):
                h, i = items[n]
                hp, par = h // 2, h % 2
                rows = slice(par * 64, par * 64 + 64)
                deltas = [d for d in range(5) if (4 * T + i - d) >= 0]
                e2 = (ecnt0 + n) % 2
                sa, rsa = PB[e2 * 2], RPB[e2 * 2]
                sb_, rsb = PB[e2 * 2 + 1], RPB[e2 * 2 + 1]
                pe.wait_for(reads=[rqt, rkt, R_bt, R_const], writes=[rsa, rsb])
                for d in deltas:
                    sl = 4 + i - d
                    dst = sa[:, d * 128:(d + 1) * 128] if d < 4 else sb_[:, 0:128]
                    hasb = d in (0, 1, 4)
                    lastd = (d == deltas[-1])
                    if hasb:
                        pe.op(lambda: P.matmul(dst, lhsT=kt[rows, hp, sl * 128:(sl + 1) * 128], rhs=qt[rows, hp, i * 128:(i + 1) * 128], start=True, stop=False),
                              reads=[rqt, rkt], inc=False)
                        bt = BT0[:, h, :] if d == 0 else (BT1[:, h, :] if d == 1 else M4[:, :])
                        pe.op(lambda: P.matmul(dst, lhsT=ident[:, :], rhs=bt, start=False, stop=True),
                              reads=[R_bt, R_const], writes=([rsa, rsb] if lastd else []), inc=lastd)
                    else:
                        pe.op(lambda: P.matmul(dst, lhsT=kt[rows, hp, sl * 128:(sl + 1) * 128], rhs=qt[rows, hp, i * 128:(i + 1) * 128], start=True, stop=True),
                              reads=[rqt, rkt], writes=([rsa, rsb] if lastd else []), inc=lastd)

            def emit_exp(n):
                h, i = items[n]
                deltas = [d for d in range(5) if (4 * T + i - d) >= 0]
                e2 = (ecnt0 + n) % 2
                sa, rsa = PB[e2 * 2], RPB[e2 * 2]
                sb_, rsb = PB[e2 * 2 + 1], RPB[e2 * 2 + 1]
                ee, ree = EE[e2], REE[e2]
                d01 = [d for d in deltas if d < 2]
                d23 = [d for d in deltas if d in (2, 3)]
                if d01:
                    act.op(lambda: A.activation(out=ee[:, 0:len(d01), :], in_=sa[:, 0:128 * len(d01)].rearrange("p (a b) -> p a b", b=128), func=AF.Exp),
                           reads=[rsa], writes=[ree])
                if d23:
                    act.op(lambda: A.activation(out=ee[:, 2:2 + len(d23), :], in_=sa[:, 256:256 + 128 * len(d23)].rearrange("p (a b) -> p a b", b=128),
                                                func=AF.Exp, bias=CF[:, h:h + 1]),
                           reads=[rsa, R_bt], writes=[ree])
                if 4 in deltas:
                    act.op(lambda: A.activation(out=ee[:, 4, :], in_=sb_[:, 0:128], func=AF.Exp, bias=CF[:, h:h + 1]),
                           reads=[rsb, R_bt], writes=[ree])

            def emit_pv(n):
                h, i = items[n]
                hp, par = h // 2, h % 2
                rows = slice(par * 64, par * 64 + 64)
                deltas = [d for d in range(5) if (4 * T + i - d) >= 0]
                e2 = (ecnt0 + n) % 2
                ee, ree = EE[e2], REE[e2]
                po, rpo = PB[4 + h % 2], RPB[4 + h % 2]
                if i == 0:
                    pe.wait_for(writes=[rpo])
                pe.wait_for(reads=[ree, rva])
                for d in deltas:
                    sl = 4 + i - d
                    lastd = (d == deltas[-1])
                    pe.op(lambda: P.matmul(po[:, i * 128:(i + 1) * 128], lhsT=va[:, sl, h, :], rhs=ee[:, d, :], start=(d == deltas[0]), stop=lastd),
                          reads=[ree, rva], writes=([rpo] if (lastd and i == 3) else []), inc=lastd)
                if i == 3:
                    rd, rrd = RD[h % 2], RRD[h % 2]
                    dve.op(lambda: V.reciprocal(out=rd[0:64, :], in_=po[64:128, :]), reads=[rpo], writes=[rrd])
                    dve.op(lambda: V.tensor_tensor(out=ast[rows, hp, :], in0=po[0:64, :], in1=rd[0:64, :], op=ALU.mult), reads=[rpo, rrd], writes=[rast])

            ecnt0 = ecnt
            emit_qk(0)
            for n in range(len(items)):
                emit_exp(n)
                if n + 1 < len(items):
                    emit_qk(n + 1)
                emit_pv(n)
            ecnt += len(items)
            sp.dma(at_d[:, t0:t0 + TT].rearrange("(kc p) t -> p kc t", p=128), ast, rast, reads=[rast])
        fw.barrier()

        def layer_norm_rows(buf, rbuf, gt, bt_, rgb, mv, stats, rstat, eps_t, gain_on_pool=False):
            for c in range(4):
                dve.op(lambda: V.bn_stats(out=stats[:, c, :], in_=buf[:, c * 512:(c + 1) * 512]), reads=[rbuf], writes=[rstat])
            dve.op(lambda: V.bn_aggr(out=mv[:, 0:2], in_=stats[:, :, :].rearrange("p c s -> p (c s)")), reads=[rstat], writes=[rstat])
            act.op(lambda: A.activation(out=mv[:, 2:3], in_=mv[:, 1:2], func=AF.Sqrt, bias=eps_t, scale=1.0), reads=[rstat, R_const], writes=[rstat])
            dve.op(lambda: V.reciprocal(out=mv[:, 2:3], in_=mv[:, 2:3]), reads=[rstat], writes=[rstat])
            dve.op(lambda: V.scalar_tensor_tensor(out=mv[:, 3:4], in0=mv[:, 0:1], scalar=-1.0, in1=mv[:, 2:3], op0=ALU.mult, op1=ALU.mult), reads=[rstat], writes=[rstat])
            act.op(lambda: A.activation(out=buf, in_=buf, func=AF.Identity, scale=mv[:, 2:3], bias=mv[:, 3:4]), reads=[rbuf, rstat], writes=[rbuf])
            if gain_on_pool:
                pool.op(lambda: G.tensor_tensor(out=buf, in0=buf, in1=gt, op=ALU.mult), reads=[rbuf, rgb], writes=[rbuf])
            else:
                dve.op(lambda: V.tensor_tensor(out=buf, in0=buf, in1=gt, op=ALU.mult), reads=[rbuf, rgb], writes=[rbuf])
            pool.op(lambda: G.tensor_tensor(out=buf, in0=buf, in1=bt_, op=ALU.add), reads=[rbuf, rgb], writes=[rbuf])

        epst = fw.sb("epst", [128, 1], F32)
        dve.op(lambda: V.memset(epst[:, :], EPS), writes=[R_const])

        arena_reset()
        G1 = carve([128, D], F32)
        B1 = carve([128, D], F32)
        BG = carve([128, 8], F32)
        GA = carve([128, 8], F32)
        GS_ = carve([128, 8], F32)
        R_par = fw.res("parM")
        sp.dma(G1, ln1g_d.partition_broadcast(128), R_par, writes=[R_par])
        sp.dma(B1, ln1b_d.partition_broadcast(128), R_par, writes=[R_par])
        sp.dma(BG, bglu_d[:, :], R_par, writes=[R_par])
        sp.dma(GA, gatt_d[:, :], R_par, writes=[R_par])
        sp.dma(GS_, gssm_d[:, :], R_par, writes=[R_par])
        ATT = [carve([128, 8, TT], BF16) for _ in range(2)]
        RATT = [fw.res(f"att{i}") for i in range(2)]
        YT = carve([128, 8, TT], BF16)
        RYT = fw.res("yt")
        SG = carve([128, 8, TT], F32)
        RSG = fw.res("sg")
        SGB = carve([128, 8, TT], BF16)
        RSGB = fw.res("sgb")
        SIG = [carve([128, TT], F32) for _ in range(2)]
        RSIG = [fw.res(f"sig{i}") for i in range(2)]
        SSMO = carve([128, 8, TT], BF16)
        RSSMO = fw.res("ssmo")
        SQ = carve([128, 16, TT], BF16)
        RSQ = fw.res("sq")
        RSTD = carve([128, 2, TT], F32)
        RRSTD = fw.res("rstd")
        MIX = carve([128, 16, TT], BF16)
        RMIX = fw.res("mix")
        XRES = [carve([128, 512], F32) for _ in range(3)]
        RXRES = [fw.res(f"xres{i}") for i in range(3)]
        PRE = carve([128, 4, D], F32)
        RPRE = [fw.res(f"pre{i}") for i in range(4)]
        MV = carve([128, 4], F32)
        STATS = carve([128, 4, 6], F32)
        RSTAT = fw.res("stat")
        xrrot = [0]
        seqM = [(w_glu_b[b], 4096) for b in range(2)]
        for T_ in range(NT):
            seqM += [(w_out_b[b], 8192) for b in range(4)]
            if T_ + 1 < NT:
                seqM += [(w_glu_b[b], 4096) for b in range(2)]
        ring.start(seqM)

        def stage_load(T):
            t0 = T * TT
            att, ratt = ATT[T % 2], RATT[T % 2]
            sp.dma(att, at_d[:, t0:t0 + TT].rearrange("(kc p) t -> p kc t", p=128), ratt, writes=[ratt])
            for ft in range(8):
                sp.dma(YT[:, ft, :].rearrange("p (j m) -> p j m", j=8), y_d[T, :, ft * 128:(ft + 1) * 128, :].rearrange("j q m -> q j m"), RYT, writes=[RYT])
            for ft in range(8):
                act.op(lambda: A.activation(out=SGB[:, ft, :].rearrange("p (m j) -> p j m", j=8), in_=YT[:, ft, :].rearrange("p (j m) -> p j m", j=8),
                                            func=AF.Gelu_apprx_tanh), reads=[RYT], writes=[RSGB])
            for ft in range(8):
                act.op(lambda: A.activation(out=SG[:, ft, :].rearrange("p (m j) -> p j m", j=8), in_=YT[:, ft, :].rearrange("p (j m) -> p j m", j=8),
                                            func=AF.Gelu_apprx_tanh), reads=[RYT], writes=[RSG])
            act.op(lambda: A.activation(out=SQ[:, 0:8, :], in_=att[:, :, :], func=AF.Square), reads=[ratt], writes=[RSQA])
            for kc in range(8):
                act.op(lambda: A.mul(out=att[:, kc, :], in_=att[:, kc, :], mul=GA[:, kc:kc + 1]), reads=[ratt, R_par], writes=[ratt])

        def front_a(T):
            for b in range(2):
                wt, rw = ring.get()
                w3 = wt[:, 0:4096].rearrange("p (k c) -> p k c", k=8)
                for ft in range(4):
                    f = b * 4 + ft
                    pb, rpb = next_pb()
                    mm_group(pb[:, :], rpb, [(w3[:, kc, ft * 128:(ft + 1) * 128], SGB[:, kc, :]) for kc in range(8)], [rw, RSGB])
                    sg_, rsg_ = SIG[f % 2], RSIG[f % 2]
                    act.op(lambda: A.activation(out=sg_, in_=pb[:, :], func=AF.Sigmoid, bias=BG[:, f:f + 1]), reads=[rpb, R_par], writes=[rsg_])
                    dve.op(lambda: V.tensor_tensor(out=SSMO[:, f, :], in0=SG[:, f, :], in1=sg_, op=ALU.mult), reads=[RSG, rsg_], writes=[RSSMO])

        def rstd_part(T, part, rsq):
            pb, rpb = next_pb()
            mm_group(pb[:, :], rpb, [(onesb[:, :], SQ[:, part * 8 + kc, :]) for kc in range(8)], [rsq, R_const])
            act.op(lambda: A.activation(out=RSTD[:, part, :], in_=pb[:, :], func=AF.Sqrt, bias=epst[:, 0:1], scale=1.0 / AW), reads=[rpb, R_const], writes=[RRSTD[part]])
            dve.op(lambda: V.reciprocal(out=RSTD[:, part, :], in_=RSTD[:, part, :]), reads=[RRSTD[part]], writes=[RRSTD[part]])

        def front_b(T):
            att, ratt = ATT[T % 2], RATT[T % 2]
            rstd_part(T, 0, RSQA)
            dve.op(lambda: V.tensor_tensor(out=MIX[:, 0:8, :], in0=att[:, :, :], in1=RSTD[:, 0:1, :].to_broadcast([128, 8, TT]), op=ALU.mult),
                   reads=[ratt, RRSTD[0]], writes=[RMIX])
            act.op(lambda: A.activation(out=SQ[:, 8:16, :], in_=SSMO[:, :, :], func=AF.Square), reads=[RSSMO], writes=[RSQS])
            for kc in range(8):
                act.op(lambda: A.mul(out=SSMO[:, kc, :], in_=SSMO[:, kc, :], mul=GS_[:, kc:kc + 1]), reads=[RSSMO, R_par], writes=[RSSMO])
            rstd_part(T, 1, RSQS)
            dve.op(lambda: V.tensor_tensor(out=MIX[:, 8:16, :], in0=SSMO[:, :, :], in1=RSTD[:, 1:2, :].to_broadcast([128, 8, TT]), op=ALU.mult),
                   reads=[RSSMO, RRSTD[1]], writes=[RMIX])

        def back_mm(T):
            t0 = T * TT
            for fb in range(4):
                wt, rw = ring.get()
                w3 = wt[:, :].rearrange("p (k c) -> p k c", k=16)
                for s in range(4):
                    pb, rpb = next_pb()
                    mm_group(pb[:, :], rpb, [(MIX[:, kc, s * 128:(s + 1) * 128], w3[:, kc, :]) for kc in range(16)], [rw, RMIX])
                    xr, rxr = XRES[xrrot[0] % 3], RXRES[xrrot[0] % 3]
                    xrrot[0] += 1
                    sp.dma(xr, x_d[t0 + s * 128:t0 + (s + 1) * 128, fb * 512:(fb + 1) * 512], rxr, writes=[rxr])
                    dve.op(lambda: V.scalar_tensor_tensor(out=PRE[:, s, fb * 512:(fb + 1) * 512], in0=xr, scalar=ALPHA, in1=pb[:, :], op0=ALU.mult, op1=ALU.add),
                           reads=[rxr, rpb], writes=[RPRE[s]])

        def back_ln(T):
            t0 = T * TT
            for s in range(4):
                layer_norm_rows(PRE[:, s, :], RPRE[s], G1, B1, R_par, MV, STATS, RSTAT, epst[:, 0:1], gain_on_pool=True)
                sp.dma(x1_d[t0 + s * 128:t0 + (s + 1) * 128, :], PRE[:, s, :], RPRE[s], reads=[RPRE[s]])

        RSQA = fw.res("sqa")
        RSQS = fw.res("sqs")
        RRSTD = [fw.res("rstd0"), fw.res("rstd1")]
        stage_load(0)
        front_a(0)
        front_b(0)
        for T in range(NT):
            if T + 1 < NT:
                stage_load(T + 1)
            back_mm(T)
            back_ln(T)
            if T + 1 < NT:
                front_a(T + 1)
                front_b(T + 1)
        ring.end()
        fw.barrier()

        arena_reset()
        G2 = carve([128, D], F32)
        B2 = carve([128, D], F32)
        CW = carve([128, 44 * 3], F32)
        CB = carve([128, 44], F32)
        HALO = carve([128, 44, 2], F32)
        R_parF = fw.res("parF")
        RHALO = fw.res("halo")
        sp.dma(G2, ln2g_d.partition_broadcast(128), R_parF, writes=[R_parF])
        sp.dma(B2, ln2b_d.partition_broadcast(128), R_parF, writes=[R_parF])
        sp.dma(CW, convw_d[:, :], R_parF, writes=[R_parF])
        sp.dma(CB, convb_d[:, :], R_parF, writes=[R_parF])
        dve.op(lambda: V.memset(HALO[:, :, :], 0.0), writes=[RHALO])
        XB = [carve([128, D], BF16) for _ in range(2)]
        RXB = [fw.res(f"x1b{i}") for i in range(2)]
        X1T = [carve([128, 16, TT], BF16) for _ in range(2)]
        RX1T = [fw.res(f"x1T{i}") for i in range(2)]
        HID = carve([128, 44, TT], BF16)
        RHID = fw.res("hid")
        GSB = [carve([128, TT + 2], F32) for _ in range(2)]
        RGSB = [fw.res(f"gsb{i}") for i in range(2)]
        CA = [carve([128, TT], F32) for _ in range(2)]
        RCA = [fw.res(f"ca{i}") for i in range(2)]
        XRES = [carve([128, 512], F32) for _ in range(3)]
        RXRES = [fw.res(f"x1res{i}") for i in range(3)]
        PRE = carve([128, 4, D], F32)
        RPRE = [fw.res(f"pre2{i}") for i in range(4)]
        MV = carve([128, 4], F32)
        STATS = carve([128, 4, 6], F32)
        RSTAT = fw.res("stat2")
        seqF = []
        for _ in range(NT):
            seqF += [(w_ffi_b[b], 8192) for b in range(22)]
            seqF += [(w_ffo_b[b], 5632) for b in range(16)]
        ring.start(seqF)
        cnt = 0
        for T in range(NT):
            t0 = T * TT
            x1T, rx1T = X1T[T % 2], RX1T[T % 2]
            if T == 0:
                load_T(x1_d, t0, XB, RXB, x1T, rx1T)
            for pr in range(22):
                wg, rwg = ring.get()
                g3 = wg[:, :].rearrange("p (k c) -> p k c", k=16)
                for f4 in range(2):
                    ft = pr * 2 + f4
                    pg, rpg = next_pb()
                    mm_group(pg[:, :], rpg, [(g3[:, kc, f4 * 128:(f4 + 1) * 128], x1T[:, kc, :]) for kc in range(16)], [rwg, rx1T])
                    pv, rpv = next_pb()
                    mm_group(pv[:, :], rpv, [(g3[:, kc, 256 + f4 * 128:256 + (f4 + 1) * 128], x1T[:, kc, :]) for kc in range(16)], [rwg, rx1T])
                    gs, rgs = GSB[cnt % 2], RGSB[cnt % 2]
                    ca, rca = CA[cnt % 2], RCA[cnt % 2]
                    cnt += 1
                    act.op(lambda: A.copy(out=gs[:, 0:2], in_=HALO[:, ft, :]), reads=[RHALO], writes=[rgs])
                    act.op(lambda: A.copy(out=gs[:, 2:TT + 2], in_=pg[:, :]), reads=[rpg], writes=[rgs])
                    act.op(lambda: A.copy(out=HALO[:, ft, :], in_=gs[:, TT:TT + 2]), reads=[rgs], writes=[RHALO])
                    dve.op(lambda: V.tensor_scalar(out=ca, in0=gs[:, 2:TT + 2], scalar1=CW[:, ft * 3 + 2:ft * 3 + 3], scalar2=CB[:, ft:ft + 1], op0=ALU.mult, op1=ALU.add),
                           reads=[rgs, R_parF], writes=[rca])
                    dve.op(lambda: V.scalar_tensor_tensor(out=ca, in0=gs[:, 1:TT + 1], scalar=CW[:, ft * 3 + 1:ft * 3 + 2], in1=ca, op0=ALU.mult, op1=ALU.add),
                           reads=[rgs, R_parF, rca], writes=[rca])
                    dve.op(lambda: V.scalar_tensor_tensor(out=ca, in0=gs[:, 0:TT], scalar=CW[:, ft * 3:ft * 3 + 1], in1=ca, op0=ALU.mult, op1=ALU.add),
                           reads=[rgs, R_parF, rca], writes=[rca])
                    act.op(lambda: A.activation(out=ca, in_=ca, func=AF.Gelu_apprx_tanh), reads=[rca], writes=[rca])
                    dve.op(lambda: V.tensor_tensor(out=HID[:, ft, :], in0=ca, in1=pv[:, :], op=ALU.mult), reads=[rca, rpv], writes=[RHID])
            if T + 1 < NT:
                load_T(x1_d, t0 + TT, XB, RXB, X1T[(T + 1) % 2], RX1T[(T + 1) % 2])
            for fb in range(4):
                banks = [next_pb() for _ in range(4)]
                for sbk in range(4):
                    wt, rw = ring.get()
                    w3 = wt[:, 0:5632].rearrange("p (k c) -> p k c", k=11)
                    for s in range(4):
                        pb, rpb = banks[s]
                        if sbk == 0:
                            pe.wait_for(writes=[rpb])
                        pe.wait_for(reads=[rw, RHID])
                        for kc in range(11):
                            fin = (sbk == 3 and kc == 10)
                            pe.op(lambda: P.matmul(pb[:, :], lhsT=HID[:, sbk * 11 + kc, s * 128:(s + 1) * 128], rhs=w3[:, kc, :],
                                                   start=(sbk == 0 and kc == 0), stop=fin),
                                  reads=[rw, RHID], writes=([rpb] if fin else []), inc=(kc == 10))
                for s in range(4):
                    pb, rpb = banks[s]
                    xr, rxr = XRES[xrrot[0] % 3], RXRES[xrrot[0] % 3]
                    xrrot[0] += 1
                    sp.dma(xr, x1_d[t0 + s * 128:t0 + (s + 1) * 128, fb * 512:(fb + 1) * 512], rxr, writes=[rxr])
                    dve.op(lambda: V.scalar_tensor_tensor(out=PRE[:, s, fb * 512:(fb + 1) * 512], in0=xr, scalar=ALPHA, in1=pb[:, :], op0=ALU.mult, op1=ALU.add),
                           reads=[rxr, rpb], writes=[RPRE[s]])
            for s in range(4):
                layer_norm_rows(PRE[:, s, :], RPRE[s], G2, B2, R_parF, MV, STATS, RSTAT, epst[:, 0:1])
                sp.dma(out_d[t0 + s * 128:t0 + (s + 1) * 128, :], PRE[:, s, :], RPRE[s], reads=[RPRE[s]])
        ring.end()
        fw.barrier()
        print("ninst", {e.name: e.ninst for e in fw.engs}, "nsem", fw.nsem)
    return nc


def _blocks(W, kc, nb):
    return np.ascontiguousarray(W.reshape(kc, 128, nb, 512).transpose(2, 1, 0, 3).reshape(nb, 128, kc * 512))


def prep_shared(inp):
    f = lambda a: np.ascontiguousarray(np.asarray(a, dtype=np.float32))
    w_in = f(inp["w_in"])[0]
    sh = {}
    wq, wk, wv, wu = w_in[:, 0:1024], w_in[:, 1024:2048], w_in[:, 2048:3072], w_in[:, 3072:4096]
    sh["w_in_b"] = np.concatenate([_blocks(np.ascontiguousarray(w), 16, 2) for w in (wq, wk, wu, wv)], axis=0)
    sh["w_glu_b"] = _blocks(f(inp["w_glu"])[0], 8, 2)
    sh["w_out_b"] = _blocks(f(inp["w_out"])[0], 16, 4)
    wfi = f(inp["w_ffn_in"])[0]
    wfi = np.concatenate([wfi[:, :DFF].reshape(D, 22, 256), wfi[:, DFF:].reshape(D, 22, 256)], axis=2).reshape(D, 22 * 512)
    sh["w_ffi_b"] = _blocks(np.ascontiguousarray(wfi), 16, 22)
    wo = f(inp["w_ffn_out"])[0]
    sh["w_ffo_b"] = np.ascontiguousarray(
        wo.reshape(4, 11, 128, 4, 512).transpose(3, 0, 2, 1, 4).reshape(16, 128, 11 * 512))
    sh["ident"] = np.eye(128, dtype=np.float32)
    jj = np.arange(128) // 16
    sh["cmask"] = (jj[None, :] >= jj[:, None]).astype(np.float32)
    kl = np.arange(128)[:, None]
    ql = np.arange(128)[None, :]
    sh["mask0"] = np.where((kl >= 64) & (ql < 64), NEGM, 0.0).astype(np.float32)
    sh["mask4"] = np.where((kl < 64) & (ql >= 64), NEGM, 0.0).astype(np.float32)
    rb = f(inp["attn_rel_bias"])[0]
    idx0 = np.clip(ql - kl, -63, 128) + 63
    idx1 = np.clip(128 + ql - kl, -63, 128) + 63
    sh["bias0"] = np.ascontiguousarray(rb[:, idx0])
    sh["bias1"] = np.ascontiguousarray(rb[:, idx1])
    sh["cfar"] = np.ascontiguousarray(np.broadcast_to(rb[:, 191][None, :], (128, NH)))

    def scan2(a):
        a = a.reshape((2, 32) + a.shape[1:])
        a = np.moveaxis(a, 2, 1)
        return np.ascontiguousarray(a.reshape((128, 32) + a.shape[3:]))

    sh["s_are"] = scan2(f(inp["ssm_a_re"])[0])
    sh["s_aim"] = scan2(f(inp["ssm_a_im"])[0])
    ldt = f(inp["ssm_log_dt"])[0]
    sh["s_ldt"] = scan2(np.ascontiguousarray(np.broadcast_to(ldt[:, None], (64, 64))))
    sh["s_bre"] = scan2(f(inp["ssm_b_re"])[0]).reshape(128, 512)
    sh["s_bim"] = scan2(f(inp["ssm_b_im"])[0]).reshape(128, 512)
    sh["s_cre"] = scan2(np.ascontiguousarray(f(inp["ssm_c_re"])[0].transpose(0, 2, 1))).reshape(128, 512)
    sh["s_cim"] = scan2(np.ascontiguousarray(f(inp["ssm_c_im"])[0].transpose(0, 2, 1))).reshape(128, 512)
    dd = f(inp["ssm_d"])[0]
    sh["s_dcol"] = np.ascontiguousarray(np.tile(dd.T, (8, 1)))
    col = lambda v: np.ascontiguousarray(v.reshape(-1, 128).T)
    sh["b_glu"] = col(f(inp["b_glu"])[0])
    sh["g_att"] = col(f(inp["g_attn_out"])[0])
    sh["g_ssm"] = col(f(inp["g_ssm_out"])[0])
    for k in ("ln1_g", "ln1_b", "ln2_g", "ln2_b"):
        sh[k] = f(inp[k])[0]
    cw = f(inp["ffn_conv_w"])[0]
    sh["conv_w"] = np.ascontiguousarray(cw.reshape(3, 44, 128).transpose(2, 1, 0).reshape(128, 132))
    sh["conv_b"] = col(f(inp["ffn_conv_b"])[0])
    return sh


_NC_CACHE = {}


def kernel(**inputs):
    x = np.asarray(inputs["x"], dtype=np.float32)
    B, L, _ = x.shape
    sh = prep_shared(inputs)
    if L not in _NC_CACHE:
        _NC_CACHE[L] = build(L)
    nc = _NC_CACHE[L]
    in_maps = [dict(sh, x=np.ascontiguousarray(x[b])) for b in range(B)]
    res = run_bass_kernel_spmd(nc, in_maps, core_ids=list(range(B)))
    return np.stack([np.asarray(r["out"], dtype=np.float32) for r in res.results], axis=0)
```

```python
import math
from contextlib import ExitStack
import numpy as np
import concourse.bass as bass
import concourse.mybir as mybir
from concourse.bass_utils import run_bass_kernel_spmd

F32 = mybir.dt.float32
BF16 = mybir.dt.bfloat16
AF = mybir.ActivationFunctionType
ALU = mybir.AluOpType

D = 2048
AW = 1024
NH = 16
DFF = 5632
TT = 512
ALPHA = 2.0 ** 0.25
EPS = 1e-5
MAGIC = 12582912.0
NEGM = -30000.0


class Res:
    __slots__ = ("name", "w", "r", "sem", "semcnt")

    def __init__(self, name=""):
        self.name = name
        self.w = None
        self.r = []
        self.sem = None
        self.semcnt = 0


class Eng:
    ROT = 6000

    def __init__(self, fw, e, name):
        self.fw = fw
        self.e = e
        self.name = name
        self.sem = None
        self.count = 0
        self.seen = {}
        self.pending = []
        self.nsem = 0
        self.ninst = 0
        self.allsems = []

    def _newsem(self):
        self.sem = self.fw.new_sem(f"{self.name}{self.nsem}")
        self.nsem += 1
        self.count = 0
        self.allsems.append(self.sem)

    def _need(self, waits, t):
        if t is None:
            return
        s, v = t[0], t[1]
        if self.seen.get(id(s), 0) >= v:
            return
        k = id(s)
        if k not in waits or waits[k][1] < v:
            waits[k] = (s, v)

    def wait_for(self, reads=(), writes=()):
        waits = {}
        for r in reads:
            self._need(waits, r.w)
        for w in writes:
            self._need(waits, w.w)
            for t in w.r:
                if t[2] is not self:
                    self._need(waits, t)
        for s, v in waits.values():
            self.e.wait_ge(s, v)
            self.seen[id(s)] = v

    def op(self, fn, reads=(), writes=(), inc=True):
        self.wait_for(reads, writes)
        ins = fn()
        self.ninst += 1
        if inc:
            if self.sem is None or self.count >= self.ROT:
                self._newsem()
            self.count += 1
            ins.then_inc(self.sem, 1)
            t = (self.sem, self.count, self)
            for r in self.pending:
                r.r.append(t)
            self.pending = []
            for r in reads:
                r.r.append(t)
            for w in writes:
                w.w = t
                w.r = []
        else:
            assert not writes
            self.pending.extend(reads)
        return ins

    def dma(self, out, in_, owner, reads=(), writes=(), **kw):
        self.wait_for(reads, writes)
        if owner.sem is None:
            owner.sem = self.fw.new_sem("d" + owner.name)
            owner.semcnt = 0
            self.fw.dma_owners.append(owner)
        owner.semcnt += 16
        ins = self.e.dma_start(out=out, in_=in_, **kw)
        ins.then_inc(owner.sem, 16)
        self.ninst += 1
        t = (owner.sem, owner.semcnt, None)
        for r in reads:
            r.r.append(t)
        for w in writes:
            w.w = t
            w.r = []
        return t


class FW:
    def __init__(self, nc, es):
        self.nc = nc
        self.es = es
        self.nsem = 0
        self.dma_owners = []
        self.allres = []
        self.pe = Eng(self, nc.tensor, "pe")
        self.act = Eng(self, nc.scalar, "act")
        self.dve = Eng(self, nc.vector, "dve")
        self.pool = Eng(self, nc.gpsimd, "pool")
        self.sp = Eng(self, nc.sync, "sp")
        self.engs = [self.pe, self.act, self.dve, self.pool, self.sp]

    def new_sem(self, name):
        self.nsem += 1
        return self.es.enter_context(self.nc.semaphore(f"s{self.nsem}_{name}"))

    def res(self, name):
        r = Res(name)
        self.allres.append(r)
        return r

    def sb(self, name, shape, dt):
        return self.es.enter_context(self.nc.sbuf_tensor(name, shape, dt))

    def ps(self, name, shape, dt):
        return self.es.enter_context(self.nc.psum_tensor(name, shape, dt))

    def barrier(self, engines=None):
        for e in self.engs:
            assert not e.pending, e.name
        ticks = []
        for e in self.engs:
            if e.sem is not None and e.count > 0:
                ticks.append((e.sem, e.count, e))
        for o in self.dma_owners:
            ticks.append((o.sem, o.semcnt, None))
        for e in (engines or self.engs):
            for s, v, src in ticks:
                if src is e:
                    continue
                if e.seen.get(id(s), 0) >= v:
                    continue
                e.e.wait_ge(s, v)
                e.seen[id(s)] = v
        if engines is None:
            for r in self.allres:
                r.w = None
                r.r = []


def build(L, dbg=False):
    NT = L // TT
    nc = bass.Bass("TRN2", target_bir_lowering=False)

    def din(name, shape, dt=F32):
        return nc.dram_tensor(name, list(shape), dt, kind="ExternalInput").ap()

    def dscr(name, shape, dt):
        kind = "ExternalOutput" if dbg else "Internal"
        return nc.dram_tensor(name, list(shape), dt, kind=kind).ap()

    x_d = din("x", [L, D])
    w_in_b = din("w_in_b", [8, 128, 8192])
    w_glu_b = din("w_glu_b", [2, 128, 4096])
    w_out_b = din("w_out_b", [4, 128, 8192])
    w_ffi_b = din("w_ffi_b", [22, 128, 8192])
    w_ffo_b = din("w_ffo_b", [16, 128, 5632])
    ident_d = din("ident", [128, 128])
    cmask_d = din("cmask", [128, 128])
    mask0_d = din("mask0", [128, 128])
    mask4_d = din("mask4", [128, 128])
    bias0_d = din("bias0", [NH, 128, 128])
    bias1_d = din("bias1", [NH, 128, 128])
    cfar_d = din("cfar", [128, NH])
    sare_d = din("s_are", [128, 32])
    saim_d = din("s_aim", [128, 32])
    sldt_d = din("s_ldt", [128, 32])
    sbre_d = din("s_bre", [128, 512])
    sbim_d = din("s_bim", [128, 512])
    scre_d = din("s_cre", [128, 512])
    scim_d = din("s_cim", [128, 512])
    sdcol_d = din("s_dcol", [128, 64])
    bglu_d = din("b_glu", [128, 8])
    gatt_d = din("g_att", [128, 8])
    gssm_d = din("g_ssm", [128, 8])
    ln1g_d = din("ln1_g", [D])
    ln1b_d = din("ln1_b", [D])
    ln2g_d = din("ln2_g", [D])
    ln2b_d = din("ln2_b", [D])
    convw_d = din("conv_w", [128, 44 * 3])
    convb_d = din("conv_b", [128, 44])
    out_d = nc.dram_tensor("out", [L, D], F32, kind="ExternalOutput").ap()

    qT_d = dscr("qT_s", [AW, L], BF16)
    kT_d = dscr("kT_s", [AW, L], BF16)
    v_d = dscr("v_s", [L, AW], BF16)
    u_d = dscr("u_s", [NT, 8, AW, 64], BF16)
    y_d = dscr("y_s", [NT, 8, AW, 64], BF16)
    at_d = dscr("at_s", [AW, L], BF16)
    x1_d = dscr("x1_s", [L, D], F32)

    es = ExitStack()
    with es:
        fw = FW(nc, es)
        pe, act, dve, pool, sp = fw.pe, fw.act, fw.dve, fw.pool, fw.sp
        V, A, P, G = nc.vector, nc.scalar, nc.tensor, nc.gpsimd

        ARENA_F32 = 40192
        arena = fw.sb("arena", [128, ARENA_F32], F32)
        apos = [0]

        def carve(shape, dt):
            n = 1
            for s_ in shape[1:]:
                n *= s_
            nf = n if dt == F32 else (n + 1) // 2
            nf = (nf + 7) // 8 * 8
            a0 = apos[0]
            assert a0 + nf <= ARENA_F32, ("arena overflow", a0, nf)
            apos[0] += nf
            v = arena[:, a0:a0 + nf]
            if dt == BF16:
                v = v.bitcast(BF16)
            v = v[:, 0:n]
            if len(shape) > 2:
                names = " ".join(f"d{i}" for i in range(1, len(shape)))
                kw = {f"d{i}": shape[i] for i in range(1, len(shape))}
                v = v.rearrange(f"p ({names}) -> p {names}", **kw)
            return v[0:shape[0]]

        def arena_reset():
            apos[0] = 0

        ident = fw.sb("identb", [128, 128], BF16)
        identf = fw.sb("identf", [128, 128], F32)
        onesb = fw.sb("onesb", [128, 128], BF16)
        R_const = fw.res("const")
        WR = [fw.sb(f"wr{i}", [128, 8192], BF16) for i in range(3)]
        RW = [fw.res(f"wr{i}") for i in range(3)]
        PB = [fw.ps(f"pb{i}", [128, 512], F32) for i in range(6)]
        RPB = [fw.res(f"pb{i}") for i in range(6)]
        PT = [fw.ps(f"pt{i}", [128, 1024], BF16) for i in range(2)]
        RPT = [fw.res(f"pt{i}") for i in range(2)]
        pbrot = [0]

        def next_pb(lo=0, hi=6):
            i = lo + pbrot[0] % (hi - lo)
            pbrot[0] += 1
            return PB[i], RPB[i]

        pool.dma(ident[:], ident_d[:, :], R_const, writes=[R_const])
        sp.dma(identf[:], ident_d[:, :], R_const, writes=[R_const])
        dve.op(lambda: V.memset(onesb[:], 1.0), writes=[R_const])

        class Ring:
            def __init__(self):
                self.seq = []
                self.issued = 0
                self.used = 0

            def start(self, seq):
                self.seq = seq
                self.issued = 0
                self.used = 0
                self.base = getattr(self, "base", 0)

            def prefetch(self, upto):
                while self.issued < min(upto, len(self.seq)):
                    src, n = self.seq[self.issued]
                    slot = (self.base + self.issued) % 3
                    pool.dma(WR[slot][:, 0:n], src, RW[slot], writes=[RW[slot]])
                    self.issued += 1

            def get(self, ahead=3):
                self.prefetch(self.used + ahead)
                slot = (self.base + self.used) % 3
                self.used += 1
                return WR[slot], RW[slot]

            def end(self):
                assert self.used == len(self.seq)
                self.base = (self.base + self.used) % 3

        ring = Ring()

        def mm_group(ps_ap, rps, pairs, reads):
            pe.wait_for(reads=reads, writes=[rps])
            n = len(pairs)
            for i, (l, r) in enumerate(pairs):
                last = (i == n - 1)
                pe.op(lambda: P.matmul(ps_ap, lhsT=l, rhs=r, start=(i == 0), stop=last),
                      reads=reads, writes=([rps] if last else []), inc=last)

        evrot = [0]

        def evac(out, in_, reads, writes, scale=None, eng=None):
            if eng is None:
                eng = "act" if (evrot[0] % 2 == 0) else "dve"
                evrot[0] += 1
            if eng == "act":
                if scale is None:
                    act.op(lambda: A.copy(out=out, in_=in_), reads=reads, writes=writes)
                else:
                    act.op(lambda: A.mul(out=out, in_=in_, mul=scale), reads=reads, writes=writes)
            else:
                if scale is None:
                    dve.op(lambda: V.tensor_copy(out=out, in_=in_), reads=reads, writes=writes)
                else:
                    dve.op(lambda: V.tensor_scalar_mul(out=out, in0=in_, scalar1=scale), reads=reads, writes=writes)

        def load_T(src_d, t0, XB, RXB, xT, RxT, eng=None):
            for s in range(4):
                xb = XB[s % 2]
                rxb = RXB[s % 2]
                pool.dma(xb[:], src_d[t0 + 128 * s: t0 + 128 * (s + 1), :], rxb, writes=[rxb])
                for half in range(2):
                    pt = PT[half]
                    pe.wait_for(reads=[rxb, R_const], writes=[RPT[half]])
                    for k in range(8):
                        kc = half * 8 + k
                        pe.op(lambda: P.transpose(pt[:, k * 128:(k + 1) * 128], xb[:, kc * 128:(kc + 1) * 128], ident[:]),
                              reads=[rxb, R_const], writes=([RPT[half]] if k == 7 else []), inc=(k == 7))
                    evac(xT[:, half * 8:half * 8 + 8, s * 128:(s + 1) * 128],
                         pt[:].rearrange("p (k c) -> p k c", k=8), [RPT[half]], [RxT], eng=eng)

        arena_reset()
        WZ = carve([128, 64, 128], BF16)
        RY = carve([128, 64, 128], BF16)
        KK = carve([128, 64, 128], BF16)
        AR2 = carve([128, 32, 2], F32)
        AIS = carve([128, 32, 2], F32)
        R_der = fw.res("derived")
        mark = apos[0]
        Rt = fw.res("dtmp")

        def small():
            return carve([128, 32], F32)

        are, aim, ldt = small(), small(), small()
        bre, bim, cre, cim = (carve([128, 32, 16], F32) for _ in range(4))
        dcol = carve([128, 64], F32)
        cmask = carve([128, 128], F32)
        halfpi = carve([128, 1], F32)
        for dst, src in ((are, sare_d), (aim, saim_d), (ldt, sldt_d), (dcol, sdcol_d), (cmask, cmask_d)):
            sp.dma(dst, src[:, :], Rt, writes=[Rt])
        for dst, src in ((bre, sbre_d), (bim, sbim_d), (cre, scre_d), (cim, scim_d)):
            sp.dma(dst, src[:, :].rearrange("p (g c) -> p g c", c=16), Rt, writes=[Rt])

        def dv(fn):
            dve.op(fn, reads=[Rt], writes=[Rt])

        def ac(fn):
            act.op(fn, reads=[Rt], writes=[Rt])

        dv(lambda: V.memset(halfpi, math.pi / 2))
        dt_, rho, th, E1, Einv, kk, thr, absx, s1, c1 = (small() for _ in range(10))
        ar1, ai1, arI, aiI, zr, den, f_re, f_im, t1, t2 = (small() for _ in range(10))
        ac(lambda: A.activation(out=dt_, in_=ldt, func=AF.Exp))
        dv(lambda: V.tensor_tensor(out=rho, in0=dt_, in1=are, op=ALU.mult))
        dv(lambda: V.tensor_tensor(out=th, in0=dt_, in1=aim, op=ALU.mult))
        ac(lambda: A.activation(out=E1, in_=rho, func=AF.Exp))
        ac(lambda: A.activation(out=Einv, in_=rho, func=AF.Exp, scale=-1.0))
        dv(lambda: V.tensor_scalar(out=kk, in0=th, scalar1=1.0 / (2 * math.pi), scalar2=MAGIC, op0=ALU.mult, op1=ALU.add))
        dv(lambda: V.tensor_scalar(out=kk, in0=kk, scalar1=MAGIC, scalar2=None, op0=ALU.subtract))
        dv(lambda: V.scalar_tensor_tensor(out=thr, in0=kk, scalar=-2 * math.pi, in1=th, op0=ALU.mult, op1=ALU.add))
        ac(lambda: A.activation(out=s1, in_=thr, func=AF.Sin))
        ac(lambda: A.activation(out=absx, in_=thr, func=AF.Abs))
        ac(lambda: A.activation(out=c1, in_=absx, func=AF.Sin, scale=-1.0, bias=halfpi))
        dv(lambda: V.tensor_tensor(out=ar1, in0=E1, in1=c1, op=ALU.mult))
        dv(lambda: V.tensor_tensor(out=ai1, in0=E1, in1=s1, op=ALU.mult))
        dv(lambda: V.tensor_tensor(out=arI, in0=Einv, in1=c1, op=ALU.mult))
        dv(lambda: V.scalar_tensor_tensor(out=aiI, in0=Einv, scalar=-1.0, in1=s1, op0=ALU.mult, op1=ALU.mult))
        dv(lambda: V.tensor_scalar(out=zr, in0=ar1, scalar1=-1.0, scalar2=None, op0=ALU.add))
        dv(lambda: V.tensor_tensor(out=den, in0=are, in1=are, op=ALU.mult))
        dv(lambda: V.tensor_tensor(out=t1, in0=aim, in1=aim, op=ALU.mult))
        dv(lambda: V.tensor_tensor(out=den, in0=den, in1=t1, op=ALU.add))
        dv(lambda: V.reciprocal(out=den, in_=den))
        dv(lambda: V.tensor_tensor(out=t1, in0=zr, in1=are, op=ALU.mult))
        dv(lambda: V.tensor_tensor(out=t2, in0=ai1, in1=aim, op=ALU.mult))
        dv(lambda: V.tensor_tensor(out=t1, in0=t1, in1=t2, op=ALU.add))
        dv(lambda: V.tensor_tensor(out=f_re, in0=t1, in1=den, op=ALU.mult))
        dv(lambda: V.tensor_tensor(out=t1, in0=ai1, in1=are, op=ALU.mult))
        dv(lambda: V.tensor_tensor(out=t2, in0=zr, in1=aim, op=ALU.mult))
        dv(lambda: V.tensor_tensor(out=t1, in0=t1, in1=t2, op=ALU.subtract))
        dv(lambda: V.tensor_tensor(out=f_im, in0=t1, in1=den, op=ALU.mult))

        def bc16(a):
            return a.unsqueeze(2).to_broadcast([128, 32, 16])

        def cmul(o_re, o_im, a_re, a_im, b_re, b_im, ta, tb, neg_im=False):
            dv(lambda: V.tensor_tensor(out=ta, in0=a_re, in1=b_re, op=ALU.mult))
            dv(lambda: V.tensor_tensor(out=tb, in0=a_im, in1=b_im, op=ALU.mult))
            dv(lambda: V.tensor_tensor(out=o_re, in0=ta, in1=tb, op=ALU.subtract))
            dv(lambda: V.tensor_tensor(out=ta, in0=a_re, in1=b_im, op=ALU.mult))
            dv(lambda: V.tensor_tensor(out=tb, in0=a_im, in1=b_re, op=ALU.mult))
            if neg_im:
                dv(lambda: V.scalar_tensor_tensor(out=o_im, in0=ta, scalar=-1.0, in1=tb, op0=ALU.mult, op1=ALU.subtract))
            else:
                dv(lambda: V.tensor_tensor(out=o_im, in0=ta, in1=tb, op=ALU.add))

        bbr = carve([128, 32, 16], F32)
        bbi = carve([128, 32, 16], F32)
        ta16 = carve([128, 32, 16], F32)
        tb16 = carve([128, 32, 16], F32)
        cmul(bbr, bbi, bc16(f_re), bc16(f_im), bre, bim, ta16, tb16)
        PWr = carve([128, 9, 32], F32)
        PWi = carve([128, 9, 32], F32)
        PIr = carve([128, 9, 32], F32)
        PIi = carve([128, 9, 32], F32)
        dv(lambda: V.tensor_copy(out=PWr[:, 1, :], in_=ar1))
        dv(lambda: V.tensor_copy(out=PWi[:, 1, :], in_=ai1))
        dv(lambda: V.tensor_copy(out=PIr[:, 1, :], in_=arI))
        dv(lambda: V.tensor_copy(out=PIi[:, 1, :], in_=aiI))
        for k in range(1, 8):
            cmul(PWr[:, k + 1, :], PWi[:, k + 1, :], PWr[:, k, :], PWi[:, k, :], ar1, ai1, t1, t2)
            cmul(PIr[:, k + 1, :], PIi[:, k + 1, :], PIr[:, k, :], PIi[:, k, :], arI, aiI, t1, t2)
        for ri in range(2):
            dve.op(lambda: V.tensor_copy(out=AR2[:, :, ri], in_=PWr[:, 8, :]), reads=[Rt], writes=[Rt, R_der])
        dve.op(lambda: V.tensor_scalar(out=AIS[:, :, 0], in0=PWi[:, 8, :], scalar1=-1.0, scalar2=None, op0=ALU.mult), reads=[Rt], writes=[Rt, R_der])
        dve.op(lambda: V.tensor_copy(out=AIS[:, :, 1], in_=PWi[:, 8, :]), reads=[Rt], writes=[Rt, R_der])
        Qr = carve([128, 32, 128], F32)
        Qi = carve([128, 32, 128], F32)
        Rr = carve([128, 32, 128], F32)
        Ri = carve([128, 32, 128], F32)
        for j in range(8):
            sl = slice(j * 16, (j + 1) * 16)
            cmul(Qr[:, :, sl], Qi[:, :, sl], bc16(PIr[:, j + 1, :]), bc16(PIi[:, j + 1, :]), bbr, bbi, ta16, tb16)
            cmul(Rr[:, :, sl], Ri[:, :, sl], cre, cim, bc16(PWr[:, j + 1, :]), bc16(PWi[:, j + 1, :]), ta16, tb16, neg_im=True)
        for gh in range(2):
            rows = slice(gh * 64, gh * 64 + 64)
            dve.op(lambda: V.tensor_copy(out=RY[0:64, gh * 32:(gh + 1) * 32, :], in_=Rr[rows, :, :]), reads=[Rt], writes=[R_der])
            dve.op(lambda: V.tensor_copy(out=RY[64:128, gh * 32:(gh + 1) * 32, :], in_=Ri[rows, :, :]), reads=[Rt], writes=[R_der])
        ktmp = carve([128, 4, 128], F32)
        Rk = fw.res("ktmp")
        for gb in range(16):
            gh = gb // 8
            rows = slice(gh * 64, gh * 64 + 64)
            pbk, rpbk = next_pb()
            pbw, rpbw = next_pb()
            pe.wait_for(reads=[Rt, R_const], writes=[rpbk, rpbw])
            for gi in range(4):
                gl = (gb % 8) * 4 + gi
                cs = slice(gi * 128, (gi + 1) * 128)
                pe.op(lambda: P.matmul(pbk[:, cs], lhsT=Qr[rows, gl, :], rhs=Rr[rows, gl, :], start=True, stop=False), reads=[Rt], inc=False)
                pe.op(lambda: P.matmul(pbk[:, cs], lhsT=Qi[rows, gl, :], rhs=Ri[rows, gl, :], start=False, stop=True),
                      reads=[Rt], writes=([rpbk] if gi == 3 else []), inc=(gi == 3))
            for gi in range(4):
                gl = (gb % 8) * 4 + gi
                pe.op(lambda: P.matmul(pbw[:, gi * 128:gi * 128 + 64], lhsT=Qr[rows, gl, :], rhs=identf[rows, gh * 64:gh * 64 + 64], start=True, stop=True),
                      reads=[Rt, R_const], inc=False)
                pe.op(lambda: P.matmul(pbw[:, gi * 128 + 64:gi * 128 + 128], lhsT=Qi[rows, gl, :], rhs=identf[rows, gh * 64:gh * 64 + 64], start=True, stop=True),
                      reads=[Rt, R_const], writes=([rpbw] if gi == 3 else []), inc=(gi == 3))
            g0 = gb * 4
            dve.op(lambda: V.tensor_tensor(out=ktmp[:, :, :], in0=pbk[:, :].rearrange("p (a b) -> p a b", a=4),
                                           in1=cmask.unsqueeze(1).to_broadcast([128, 4, 128]), op=ALU.mult),
                   reads=[rpbk, Rt], writes=[Rk])
            for gi in range(4):
                dve.op(lambda: V.scalar_tensor_tensor(out=KK[:, g0 + gi, :], in0=identf[:, :], scalar=dcol[:, g0 + gi:g0 + gi + 1],
                                                      in1=ktmp[:, gi, :], op0=ALU.mult, op1=ALU.add),
                       reads=[Rk, Rt, R_const], writes=[R_der])
            act.op(lambda: A.copy(out=WZ[:, g0:g0 + 4, :], in_=pbw[:, :].rearrange("p (a b) -> p a b", a=4)), reads=[rpbw], writes=[R_der])
        fw.barrier()
        apos[0] = mark
        XB = [carve([128, D], BF16) for _ in range(1)] * 2
        RXB = [fw.res("xb0")] * 2
        XT = [carve([128, 16, TT], BF16) for _ in range(2)]
        RXT = [fw.res(f"xT{i}") for i in range(2)]
        STG = [carve([128, TT], BF16) for _ in range(3)]
        RSTG = [fw.res(f"stg{i}") for i in range(3)]
        stgrot = [0]

        def next_stg():
            i = stgrot[0] % 3
            stgrot[0] += 1
            return STG[i], RSTG[i]

        UG = [carve([128, 64, 64], BF16) for _ in range(2)]
        RUG = [fw.res(f"ug{i}") for i in range(2)]
        ZSB = carve([128, 32, 2, 64], F32)
        RZ = fw.res("zsb")
        SH = carve([128, 65, 32, 2], F32)
        RSH = fw.res("sh")
        RSHh = [fw.res("sh0"), fw.res("sh1")]
        SQ2 = carve([128, 64, 64], BF16)
        RSQ2 = fw.res("sq2")
        YSB = [carve([128, 64, 64], BF16) for _ in range(1)]
        RYSB = [fw.res(f"ysb{i}") for i in range(1)]
        TS = carve([128, 32, 2], F32)
        P1 = carve([128, 32, 2], F32)
        P2 = carve([128, 32, 2], F32)
        dve.op(lambda: V.memset(SH[:, 0, :, :], 0.0), writes=[RSH, RSHh[0], RSHh[1]])
        R_UD = [[fw.res(f"ud{t}_{k}") for k in range(8)] for t in range(NT)]
        ring.start([(w_in_b[b], 8192) for _ in range(NT) for b in range(8)])

        def proj_tile(T, hook=None):
            t0 = T * TT
            xT = XT[T % 2]
            RxT = RXT[T % 2]
            if T == 0:
                load_T(x_d, t0, XB, RXB, xT, RxT, eng="act")
            for b in range(8):
                if b == 2 and hook is not None:
                    hook()
                if b == 6:
                    load_ug(T)
                if b == 6 and T + 1 < NT:
                    load_T(x_d, t0 + TT, XB, RXB, XT[(T + 1) % 2], RXT[(T + 1) % 2], eng="act")
                wt, rw = ring.get()
                w3 = wt[:, :].rearrange("p (k c) -> p k c", k=16)
                if b < 6:
                    for ft in range(4):
                        pb, rpb = next_pb()
                        mm_group(pb[:, :], rpb,
                                 [(w3[:, kc, ft * 128:(ft + 1) * 128], xT[:, kc, :]) for kc in range(16)],
                                 [rw, RxT])
                        st, rst = next_stg()
                        f0 = (b % 2) * 512 + ft * 128
                        if b < 2:
                            evac(st[:, :], pb[:, :], [rpb], [rst], scale=0.125, eng="act")
                            sp.dma(qT_d[f0:f0 + 128, t0:t0 + TT], st[:, :], rst, reads=[rst])
                        elif b < 4:
                            evac(st[:, :], pb[:, :], [rpb], [rst], eng="act")
                            sp.dma(kT_d[f0:f0 + 128, t0:t0 + TT], st[:, :], rst, reads=[rst])
                        else:
                            evac(st[:, :].rearrange("p (j m) -> p j m", j=8),
                                 pb[:, :].rearrange("p (m j) -> p j m", j=8), [rpb], [rst], eng="act")
                            sp.dma(u_d[T, :, f0:f0 + 128, :].rearrange("j q m -> q j m"),
                                   st[:, :].rearrange("p (j m) -> p j m", j=8), rst, reads=[rst], writes=[R_UD[T][(b - 4) * 4 + ft]])
                else:
                    for s in range(4):
                        pb, rpb = next_pb()
                        mm_group(pb[:, :], rpb,
                                 [(xT[:, kc, s * 128:(s + 1) * 128], w3[:, kc, :]) for kc in range(16)],
                                 [rw, RxT])
                        st, rst = next_stg()
                        evac(st[:, :], pb[:, :], [rpb], [rst], eng="act")
                        c0 = (b - 6) * 512
                        sp.dma(v_d[t0 + s * 128:t0 + (s + 1) * 128, c0:c0 + 512], st[:, :], rst, reads=[rst])

        def load_ug(T):
            ug, rug = UG[T % 2], RUG[T % 2]
            for j in range(8):
                sp.dma(ug[16 * j:16 * (j + 1), :, :], u_d[T, j].rearrange("(g p) m -> p g m", p=16), rug, reads=R_UD[T], writes=[rug])

        def ssm_front(T):
            ug, rug = UG[T % 2], RUG[T % 2]
            for gb in range(16):
                pb, rpb = next_pb()
                pe.wait_for(reads=[rug, R_der], writes=[rpb])
                for gi in range(4):
                    g = gb * 4 + gi
                    for ri in range(2):
                        last = (gi == 3 and ri == 1)
                        c0 = (gi * 2 + ri) * 64
                        pe.op(lambda: P.matmul(pb[0:64, c0:c0 + 64], lhsT=WZ[:, g, ri * 64:(ri + 1) * 64], rhs=ug[:, g, :], start=True, stop=True),
                              reads=[rug, R_der], writes=([rpb] if last else []), inc=last)
                gh = gb // 8
                gl0 = (gb % 8) * 4
                evac(ZSB[gh * 64:gh * 64 + 64, gl0:gl0 + 4, :, :],
                     pb[0:64, :].rearrange("p (a b c) -> p a b c", a=4, b=2), [rpb], [RZ], eng="act")
            halves = ((RSHh[0], slice(0, 64)), (RSHh[1], slice(64, 128)))
            for m in range(64):
                for (rs_, hs) in halves:
                    dve.op(lambda: V.tensor_tensor(out=TS[hs, :, :], in0=SH[hs, m, :, :], in1=ZSB[hs, :, :, m], op=ALU.add), reads=[rs_, RZ], writes=[rs_])
                for (rs_, hs) in halves:
                    dve.op(lambda: V.tensor_tensor(out=P1[hs, :, :], in0=TS[hs, :, :], in1=AR2[hs, :, :], op=ALU.mult), reads=[rs_, R_der], writes=[rs_])
                for (rs_, hs) in halves:
                    dve.op(lambda: V.tensor_tensor(out=P2[hs, :, :], in0=TS[hs, :, ::-1], in1=AIS[hs, :, :], op=ALU.mult), reads=[rs_, R_der], writes=[rs_])
                for (rs_, hs) in halves:
                    dve.op(lambda: V.tensor_tensor(out=SH[hs, m + 1, :, :], in0=P1[hs, :, :], in1=P2[hs, :, :], op=ALU.add), reads=[rs_], writes=[rs_])

        def ssm_back_a(T):
            ug, rug = UG[T % 2], RUG[T % 2]
            for gh in range(2):
                for ri in range(2):
                    act.op(lambda: A.copy(out=SQ2[ri * 64:(ri + 1) * 64, gh * 32:(gh + 1) * 32, :],
                                          in_=SH[gh * 64:(gh + 1) * 64, 0:64, :, ri].rearrange("p m g -> p g m")),
                           reads=[RSH, RSHh[0], RSHh[1]], writes=[RSQ2])
            dve.op(lambda: V.tensor_copy(out=SH[:, 0, :, :], in_=SH[:, 64, :, :]), reads=[RSH, RSQ2, RSHh[0], RSHh[1]], writes=[RSH, RSHh[0], RSHh[1]])

        def ssm_back_b(T):
            ug, rug = UG[T % 2], RUG[T % 2]
            ysb, rysb = YSB[0], RYSB[0]
            for gb in range(8):
                pb, rpb = next_pb()
                pe.wait_for(reads=[rug, R_der, RSQ2], writes=[rpb])
                for gi in range(8):
                    g = gb * 8 + gi
                    cs = slice(gi * 64, (gi + 1) * 64)
                    pe.op(lambda: P.matmul(pb[:, cs], lhsT=RY[:, g, :], rhs=SQ2[:, g, :], start=True, stop=False), reads=[R_der, RSQ2], inc=False)
                    pe.op(lambda: P.matmul(pb[:, cs], lhsT=KK[:, g, :], rhs=ug[:, g, :], start=False, stop=True),
                          reads=[R_der, rug], writes=([rpb] if gi == 7 else []), inc=(gi == 7))
                evac(ysb[:, gb * 8:(gb + 1) * 8, :], pb[:, :].rearrange("p (a b) -> p a b", a=8), [rpb], [rysb], eng="act")
            for j in range(8):
                sp.dma(y_d[T, j].rearrange("(g q) m -> q g m", q=16), ysb[16 * j:16 * (j + 1), :, :], rysb, reads=[rysb])


        for T in range(NT + 2):
            if 1 <= T <= NT:
                ssm_front(T - 1)
            hook = (lambda t=T: ssm_back_b(t - 2)) if T >= 2 else None
            if T < NT:
                proj_tile(T, hook=hook)
            elif hook is not None:
                hook()
            if 1 <= T <= NT:
                ssm_back_a(T - 1)
        ring.end()
        fw.barrier()

        arena_reset()
        BT0 = carve([128, NH, 128], BF16)
        BT1 = carve([128, NH, 128], BF16)
        M4 = carve([128, 128], BF16)
        CF = carve([128, NH], F32)
        R_bt = fw.res("bt")
        mark = apos[0]
        btmp = carve([128, NH, 128], F32)
        mtmp = carve([128, 128], F32)
        Rb = fw.res("btmp")
        sp.dma(btmp, bias0_d.rearrange("h k q -> k h q"), Rb, writes=[Rb])
        sp.dma(mtmp, mask0_d[:, :], Rb, writes=[Rb])
        sp.dma(CF, cfar_d[:, :], R_bt, writes=[R_bt])
        dve.op(lambda: V.tensor_tensor(out=BT0[:, :, :], in0=btmp[:, :, :], in1=mtmp.unsqueeze(1).to_broadcast([128, NH, 128]), op=ALU.add),
               reads=[Rb], writes=[R_bt])
        sp.dma(btmp, bias1_d.rearrange("h k q -> k h q"), Rb, writes=[Rb])
        sp.dma(mtmp, mask4_d[:, :], Rb, writes=[Rb])
        dve.op(lambda: V.tensor_copy(out=BT1[:, :, :], in_=btmp[:, :, :]), reads=[Rb], writes=[R_bt])
        dve.op(lambda: V.tensor_copy(out=M4[:, :], in_=mtmp[:, :]), reads=[Rb], writes=[R_bt])
        fw.barrier()
        apos[0] = mark
        QT = [carve([128, 8, TT], BF16) for _ in range(2)]
        RQT = [fw.res(f"qt{i}") for i in range(2)]
        KT = [carve([128, 8, 2 * TT], BF16) for _ in range(2)]
        RKT = [fw.res(f"kt{i}") for i in range(2)]
        VA = [carve([128, 8, NH, 128], BF16) for _ in range(2)]
        RVA = [fw.res(f"va{i}") for i in range(2)]
        EE = [carve([128, 5, 128], BF16) for _ in range(2)]
        REE = [fw.res(f"ee{i}") for i in range(2)]
        AST = [carve([128, 8, TT], BF16) for _ in range(2)]
        RAST = [fw.res(f"ast{i}") for i in range(2)]
        RD = [carve([128, TT], F32) for _ in range(2)]
        RRD = [fw.res(f"rd{i}") for i in range(2)]
        for i in range(2):
            dve.op(lambda: V.memset(VA[i][:, :, :, 64:128], 1.0), writes=[RVA[i]])
        ecnt = 0
        for T in range(NT):
            t0 = T * TT
            qt, rqt = QT[T % 2], RQT[T % 2]
            kt, rkt = KT[T % 2], RKT[T % 2]
            va, rva = VA[T % 2], RVA[T % 2]
            ast, rast = AST[T % 2], RAST[T % 2]
            sp.dma(qt, qT_d[:, t0:t0 + TT].rearrange("(kc p) t -> p kc t", p=128), rqt, writes=[rqt])
            if T == 0:
                sp.dma(kt[:, :, TT:2 * TT], kT_d[:, 0:TT].rearrange("(kc p) t -> p kc t", p=128), rkt, writes=[rkt])
            else:
                sp.dma(kt, kT_d[:, t0 - TT:t0 + TT].rearrange("(kc p) t -> p kc t", p=128), rkt, writes=[rkt])
            for sl in range(8):
                k0 = t0 - TT + sl * 128
                if k0 < 0:
                    continue
                sp.dma(va[:, sl, :, 0:64], v_d[k0:k0 + 128, :].rearrange("t (h d) -> t h d", h=NH), rva, writes=[rva])
            items = [(h, i) for h in range(NH) for i in range(4)]

            def emit_qk(n):
                h, i = items[n]
                hp, par = h // 2, h % 2
                rows = slice(par * 64, par * 64 + 64)
                deltas = [d for d in range(5) if (4 * T + i - d) >= 0]
                e2 = (ecnt0 + n) % 2
                sa, rsa = PB[e2 * 2], RPB[e2 * 2]
                sb_, rsb = PB[e2 * 2 + 1], RPB[e2 * 2 + 1]
                pe.wait_for(reads=[rqt, rkt, R_bt, R_const], writes=[rsa, rsb])
                for d in deltas:
                    sl = 4 + i - d
                    dst = sa[:, d * 128:(d + 1) * 128] if d < 4 else sb_[:, 0:128]
                    hasb = d in (0, 1, 4)
                    lastd = (d == deltas[-1])
                    if hasb:
                        pe.op(lambda: P.matmul(dst, lhsT=kt[rows, hp, sl * 128:(sl + 1) * 128], rhs=qt[rows, hp, i * 128:(i + 1) * 128], start=True, stop=False),
                              reads=[rqt, rkt], inc=False)
                        bt = BT0[:, h, :] if d == 0 else (BT1[:, h, :] if d == 1 else M4[:, :])
                        pe.op(lambda: P.matmul(dst, lhsT=ident[:, :], rhs=bt, start=False, stop=True),
                              reads=[R_bt, R_const], writes=([rsa, rsb] if lastd else []), inc=lastd)
                    else:
                        pe.op(lambda: P.matmul(dst, lhsT=kt[rows, hp, sl * 128:(sl + 1) * 128], rhs=qt[rows, hp, i * 128:(i + 1) * 128], start=True, stop=True),
                              reads=[rqt, rkt], writes=([rsa, rsb] if lastd else []), inc=lastd)

            def emit_exp(n):
                h, i = items[n]
                deltas = [d for d in range(5) if (4 * T + i - d) >= 0]
                e2 = (ecnt0 + n) % 2
                sa, rsa = PB[e2 * 2], RPB[e2 * 2]
                sb_, rsb = PB[e2 * 2 + 1], RPB[e2 * 2 + 1]
                ee, ree = EE[e2], REE[e2]
                d01 = [d for d in deltas if d < 2]
                d23 = [d for d in deltas if d in (2, 3)]
                if d01:
                    act.op(lambda: A.activation(out=ee[:, 0:len(d01), :], in_=sa[:, 0:128 * len(d01)].rearrange("p (a b) -> p a b", b=128), func=AF.Exp),
                           reads=[rsa], writes=[ree])
                if d23:
                    act.op(lambda: A.activation(out=ee[:, 2:2 + len(d23), :], in_=sa[:, 256:256 + 128 * len(d23)].rearrange("p (a b) -> p a b", b=128),
                                                func=AF.Exp, bias=CF[:, h:h + 1]),
                           reads=[rsa, R_bt], writes=[ree])
                if 4 in deltas:
                    act.op(lambda: A.activation(out=ee[:, 4, :], in_=sb_[:, 0:128], func=AF.Exp, bias=CF[:, h:h + 1]),
                           reads=[rsb, R_bt], writes=[ree])

            def emit_pv(n):
                h, i = items[n]
                hp, par = h // 2, h % 2
                rows = slice(par * 64, par * 64 + 64)
                deltas = [d for d in range(5) if (4 * T + i - d) >= 0]
                e2 = (ecnt0 + n) % 2
                ee, ree = EE[e2], REE[e2]
                po, rpo = PB[4 + h % 2], RPB[4 + h % 2]
                if i == 0:
                    pe.wait_for(writes=[rpo])
                pe.wait_for(reads=[ree, rva])
                for d in deltas:
                    sl = 4 + i - d
                    lastd = (d == deltas[-1])
                    pe.op(lambda: P.matmul(po[:, i * 128:(i + 1) * 128], lhsT=va[:, sl, h, :], rhs=ee[:, d, :], start=(d == deltas[0]), stop=lastd),
                          reads=[ree, rva], writes=([rpo] if (lastd and i == 3) else []), inc=lastd)
                if i == 3:
                    rd, rrd = RD[h % 2], RRD[h % 2]
                    dve.op(lambda: V.reciprocal(out=rd[0:64, :], in_=po[64:128, :]), reads=[rpo], writes=[rrd])
                    dve.op(lambda: V.tensor_tensor(out=ast[rows, hp, :], in0=po[0:64, :], in1=rd[0:64, :], op=ALU.mult), reads=[rpo, rrd], writes=[rast])

            ecnt0 = ecnt
            emit_qk(0)
            for n in range(len(items)):
                emit_exp(n)
                if n + 1 < len(items):
                    emit_qk(n + 1)
                emit_pv(n)
            ecnt += len(items)
            sp.dma(at_d[:, t0:t0 + TT].rearrange("(kc p) t -> p kc t", p=128), ast, rast, reads=[rast])
        fw.barrier()

        def layer_norm_rows(buf, rbuf, gt, bt_, rgb, mv, stats, rstat, eps_t, gain_on_pool=False):
            for c in range(4):
                dve.op(lambda: V.bn_stats(out=stats[:, c, :], in_=buf[:, c * 512:(c + 1) * 512]), reads=[rbuf], writes=[rstat])
            dve.op(lambda: V.bn_aggr(out=mv[:, 0:2], in_=stats[:, :, :].rearrange("p c s -> p (c s)")), reads=[rstat], writes=[rstat])
            act.op(lambda: A.activation(out=mv[:, 2:3], in_=mv[:, 1:2], func=AF.Sqrt, bias=eps_t, scale=1.0), reads=[rstat, R_const], writes=[rstat])
            dve.op(lambda: V.reciprocal(out=mv[:, 2:3], in_=mv[:, 2:3]), reads=[rstat], writes=[rstat])
            dve.op(lambda: V.scalar_tensor_tensor(out=mv[:, 3:4], in0=mv[:, 0:1], scalar=-1.0, in1=mv[:, 2:3], op0=ALU.mult, op1=ALU.mult), reads=[rstat], writes=[rstat])
            act.op(lambda: A.activation(out=buf, in_=buf, func=AF.Identity, scale=mv[:, 2:3], bias=mv[:, 3:4]), reads=[rbuf, rstat], writes=[rbuf])
            if gain_on_pool:
                pool.op(lambda: G.tensor_tensor(out=buf, in0=buf, in1=gt, op=ALU.mult), reads=[rbuf, rgb], writes=[rbuf])
            else:
                dve.op(lambda: V.tensor_tensor(out=buf, in0=buf, in1=gt, op=ALU.mult), reads=[rbuf, rgb], writes=[rbuf])
            pool.op(lambda: G.tensor_tensor(out=buf, in0=buf, in1=bt_, op=ALU.add), reads=[rbuf, rgb], writes=[rbuf])

        epst = fw.sb("epst", [128, 1], F32)
        dve.op(lambda: V.memset(epst[:, :], EPS), writes=[R_const])

        arena_reset()
        G1 = carve([128, D], F32)
        B1 = carve([128, D], F32)
        BG = carve([128, 8], F32)
        GA = carve([128, 8], F32)
        GS_ = carve([128, 8], F32)
        R_par = fw.res("parM")
        sp.dma(G1, ln1g_d.partition_broadcast(128), R_par, writes=[R_par])
        sp.dma(B1, ln1b_d.partition_broadcast(128), R_par, writes=[R_par])
        sp.dma(BG, bglu_d[:, :], R_par, writes=[R_par])
        sp.dma(GA, gatt_d[:, :], R_par, writes=[R_par])
        sp.dma(GS_, gssm_d[:, :], R_par, writes=[R_par])
        ATT = [carve([128, 8, TT], BF16) for _ in range(2)]
        RATT = [fw.res(f"att{i}") for i in range(2)]
        YT = carve([128, 8, TT], BF16)
        RYT = fw.res("yt")
        SG = carve([128, 8, TT], F32)
        RSG = fw.res("sg")
        SGB = carve([128, 8, TT], BF16)
        RSGB = fw.res("sgb")
        SIG = [carve([128, TT], F32) for _ in range(2)]
        RSIG = [fw.res(f"sig{i}") for i in range(2)]
        SSMO = carve([128, 8, TT], BF16)
        RSSMO = fw.res("ssmo")
        SQ = carve([128, 16, TT], BF16)
        RSQ = fw.res("sq")
        RSTD = carve([128, 2, TT], F32)
        RRSTD = fw.res("rstd")
        MIX = carve([128, 16, TT], BF16)
        RMIX = fw.res("mix")
        XRES = [carve([128, 512], F32) for _ in range(3)]
        RXRES = [fw.res(f"xres{i}") for i in range(3)]
        PRE = carve([128, 4, D], F32)
        RPRE = [fw.res(f"pre{i}") for i in range(4)]
        MV = carve([128, 4], F32)
        STATS = carve([128, 4, 6], F32)
        RSTAT = fw.res("stat")
        xrrot = [0]
        seqM = [(w_glu_b[b], 4096) for b in range(2)]
        for T_ in range(NT):
            seqM += [(w_out_b[b], 8192) for b in range(4)]
            if T_ + 1 < NT:
                seqM += [(w_glu_b[b], 4096) for b in range(2)]
        ring.start(seqM)

        def stage_load(T):
            t0 = T * TT
            att, ratt = ATT[T % 2], RATT[T % 2]
            sp.dma(att, at_d[:, t0:t0 + TT].rearrange("(kc p) t -> p kc t", p=128), ratt, writes=[ratt])
            for ft in range(8):
                sp.dma(YT[:, ft, :].rearrange("p (j m) -> p j m", j=8), y_d[T, :, ft * 128:(ft + 1) * 128, :].rearrange("j q m -> q j m"), RYT, writes=[RYT])
            for ft in range(8):
                act.op(lambda: A.activation(out=SGB[:, ft, :].rearrange("p (m j) -> p j m", j=8), in_=YT[:, ft, :].rearrange("p (j m) -> p j m", j=8),
                                            func=AF.Gelu_apprx_tanh), reads=[RYT], writes=[RSGB])
            for ft in range(8):
                act.op(lambda: A.activation(out=SG[:, ft, :].rearrange("p (m j) -> p j m", j=8), in_=YT[:, ft, :].rearrange("p (j m) -> p j m", j=8),
                                            func=AF.Gelu_apprx_tanh), reads=[RYT], writes=[RSG])
            act.op(lambda: A.activation(out=SQ[:, 0:8, :], in_=att[:, :, :], func=AF.Square), reads=[ratt], writes=[RSQA])
            for kc in range(8):
                act.op(lambda: A.mul(out=att[:, kc, :], in_=att[:, kc, :], mul=GA[:, kc:kc + 1]), reads=[ratt, R_par], writes=[ratt])

        def front_a(T):
            for b in range(2):
                wt, rw = ring.get()
                w3 = wt[:, 0:4096].rearrange("p (k c) -> p k c", k=8)
                for ft in range(4):
                    f = b * 4 + ft
                    pb, rpb = next_pb()
                    mm_group(pb[:, :], rpb, [(w3[:, kc, ft * 128:(ft + 1) * 128], SGB[:, kc, :]) for kc in range(8)], [rw, RSGB])
                    sg_, rsg_ = SIG[f % 2], RSIG[f % 2]
                    act.op(lambda: A.activation(out=sg_, in_=pb[:, :], func=AF.Sigmoid, bias=BG[:, f:f + 1]), reads=[rpb, R_par], writes=[rsg_])
                    dve.op(lambda: V.tensor_tensor(out=SSMO[:, f, :], in0=SG[:, f, :], in1=sg_, op=ALU.mult), reads=[RSG, rsg_], writes=[RSSMO])

        def rstd_part(T, part, rsq):
            pb, rpb = next_pb()
            mm_group(pb[:, :], rpb, [(onesb[:, :], SQ[:, part * 8 + kc, :]) for kc in range(8)], [rsq, R_const])
            act.op(lambda: A.activation(out=RSTD[:, part, :], in_=pb[:, :], func=AF.Sqrt, bias=epst[:, 0:1], scale=1.0 / AW), reads=[rpb, R_const], writes=[RRSTD[part]])
            dve.op(lambda: V.reciprocal(out=RSTD[:, part, :], in_=RSTD[:, part, :]), reads=[RRSTD[part]], writes=[RRSTD[part]])

        def front_b(T):
            att, ratt = ATT[T % 2], RATT[T % 2]
            rstd_part(T, 0, RSQA)
            dve.op(lambda: V.tensor_tensor(out=MIX[:, 0:8, :], in0=att[:, :, :], in1=RSTD[:, 0:1, :].to_broadcast([128, 8, TT]), op=ALU.mult),
                   reads=[ratt, RRSTD[0]], writes=[RMIX])
            act.op(lambda: A.activation(out=SQ[:, 8:16, :], in_=SSMO[:, :, :], func=AF.Square), reads=[RSSMO], writes=[RSQS])
            for kc in range(8):
                act.op(lambda: A.mul(out=SSMO[:, kc, :], in_=SSMO[:, kc, :], mul=GS_[:, kc:kc + 1]), reads=[RSSMO, R_par], writes=[RSSMO])
            rstd_part(T, 1, RSQS)
            dve.op(lambda: V.tensor_tensor(out=MIX[:, 8:16, :], in0=SSMO[:, :, :], in1=RSTD[:, 1:2, :].to_broadcast([128, 8, TT]), op=ALU.mult),
                   reads=[RSSMO, RRSTD[1]], writes=[RMIX])

        def back_mm(T):
            t0 = T * TT
            for fb in range(4):
                wt, rw = ring.get()
                w3 = wt[:, :].rearrange("p (k c) -> p k c", k=16)
                for s in range(4):
                    pb, rpb = next_pb()
                    mm_group(pb[:, :], rpb, [(MIX[:, kc, s * 128:(s + 1) * 128], w3[:, kc, :]) for kc in range(16)], [rw, RMIX])
                    xr, rxr = XRES[xrrot[0] % 3], RXRES[xrrot[0] % 3]
                    xrrot[0] += 1
                    sp.dma(xr, x_d[t0 + s * 128:t0 + (s + 1) * 128, fb * 512:(fb + 1) * 512], rxr, writes=[rxr])
                    dve.op(lambda: V.scalar_tensor_tensor(out=PRE[:, s, fb * 512:(fb + 1) * 512], in0=xr, scalar=ALPHA, in1=pb[:, :], op0=ALU.mult, op1=ALU.add),
                           reads=[rxr, rpb], writes=[RPRE[s]])

        def back_ln(T):
            t0 = T * TT
            for s in range(4):
                layer_norm_rows(PRE[:, s, :], RPRE[s], G1, B1, R_par, MV, STATS, RSTAT, epst[:, 0:1], gain_on_pool=True)
                sp.dma(x1_d[t0 + s * 128:t0 + (s + 1) * 128, :], PRE[:, s, :], RPRE[s], reads=[RPRE[s]])

        RSQA = fw.res("sqa")
        RSQS = fw.res("sqs")
        RRSTD = [fw.res("rstd0"), fw.res("rstd1")]
        stage_load(0)
        front_a(0)
        front_b(0)
        for T in range(NT):
            if T + 1 < NT:
                stage_load(T + 1)
            back_mm(T)
            back_ln(T)
            if T + 1 < NT:
                front_a(T + 1)
                front_b(T + 1)
        ring.end()
        fw.barrier()

        arena_reset()
        G2 = carve([128, D], F32)
        B2 = carve([128, D], F32)
        CW = carve([128, 44 * 3], F32)
        CB = carve([128, 44], F32)
        HALO = carve([128, 44, 2], F32)
        R_parF = fw.res("parF")
        RHALO = fw.res("halo")
        sp.dma(G2, ln2g_d.partition_broadcast(128), R_parF, writes=[R_parF])
        sp.dma(B2, ln2b_d.partition_broadcast(128), R_parF, writes=[R_parF])
        sp.dma(CW, convw_d[:, :], R_parF, writes=[R_parF])
        sp.dma(CB, convb_d[:, :], R_parF, writes=[R_parF])
        dve.op(lambda: V.memset(HALO[:, :, :], 0.0), writes=[RHALO])
        XB = [carve([128, D], BF16) for _ in range(2)]
        RXB = [fw.res(f"x1b{i}") for i in range(2)]
        X1T = [carve([128, 16, TT], BF16) for _ in range(2)]
        RX1T = [fw.res(f"x1T{i}") for i in range(2)]
        HID = carve([128, 44, TT], BF16)
        RHID = fw.res("hid")
        GSB = [carve([128, TT + 2], F32) for _ in range(2)]
        RGSB = [fw.res(f"gsb{i}") for i in range(2)]
        CA = [carve([128, TT], F32) for _ in range(2)]
        RCA = [fw.res(f"ca{i}") for i in range(2)]
        XRES = [carve([128, 512], F32) for _ in range(3)]
        RXRES = [fw.res(f"x1res{i}") for i in range(3)]
        PRE = carve([128, 4, D], F32)
        RPRE = [fw.res(f"pre2{i}") for i in range(4)]
        MV = carve([128, 4], F32)
        STATS = carve([128, 4, 6], F32)
        RSTAT = fw.res("stat2")
        seqF = []
        for _ in range(NT):
            seqF += [(w_ffi_b[b], 8192) for b in range(22)]
            seqF += [(w_ffo_b[b], 5632) for b in range(16)]
        ring.start(seqF)
        cnt = 0
        for T in range(NT):
            t0 = T * TT
            x1T, rx1T = X1T[T % 2], RX1T[T % 2]
            if T == 0:
                load_T(x1_d, t0, XB, RXB, x1T, rx1T)
            for pr in range(22):
                wg, rwg = ring.get()
                g3 = wg[:, :].rearrange("p (k c) -> p k c", k=16)
                for f4 in range(2):
                    ft = pr * 2 + f4
                    pg, rpg = next_pb()
                    mm_group(pg[:, :], rpg, [(g3[:, kc, f4 * 128:(f4 + 1) * 128], x1T[:, kc, :]) for kc in range(16)], [rwg, rx1T])
                    pv, rpv = next_pb()
                    mm_group(pv[:, :], rpv, [(g3[:, kc, 256 + f4 * 128:256 + (f4 + 1) * 128], x1T[:, kc, :]) for kc in range(16)], [rwg, rx1T])
                    gs, rgs = GSB[cnt % 2], RGSB[cnt % 2]
                    ca, rca = CA[cnt % 2], RCA[cnt % 2]
                    cnt += 1
                    act.op(lambda: A.copy(out=gs[:, 0:2], in_=HALO[:, ft, :]), reads=[RHALO], writes=[rgs])
                    act.op(lambda: A.copy(out=gs[:, 2:TT + 2], in_=pg[:, :]), reads=[rpg], writes=[rgs])
                    act.op(lambda: A.copy(out=HALO[:, ft, :], in_=gs[:, TT:TT + 2]), reads=[rgs], writes=[RHALO])
                    dve.op(lambda: V.tensor_scalar(out=ca, in0=gs[:, 2:TT + 2], scalar1=CW[:, ft * 3 + 2:ft * 3 + 3], scalar2=CB[:, ft:ft + 1], op0=ALU.mult, op1=ALU.add),
                           reads=[rgs, R_parF], writes=[rca])
                    dve.op(lambda: V.scalar_tensor_tensor(out=ca, in0=gs[:, 1:TT + 1], scalar=CW[:, ft * 3 + 1:ft * 3 + 2], in1=ca, op0=ALU.mult, op1=ALU.add),
                           reads=[rgs, R_parF, rca], writes=[rca])
                    dve.op(lambda: V.scalar_tensor_tensor(out=ca, in0=gs[:, 0:TT], scalar=CW[:, ft * 3:ft * 3 + 1], in1=ca, op0=ALU.mult, op1=ALU.add),
                           reads=[rgs, R_parF, rca], writes=[rca])
                    act.op(lambda: A.activation(out=ca, in_=ca, func=AF.Gelu_apprx_tanh), reads=[rca], writes=[rca])
                    dve.op(lambda: V.tensor_tensor(out=HID[:, ft, :], in0=ca, in1=pv[:, :], op=ALU.mult), reads=[rca, rpv], writes=[RHID])
            if T + 1 < NT:
                load_T(x1_d, t0 + TT, XB, RXB, X1T[(T + 1) % 2], RX1T[(T + 1) % 2])
            for fb in range(4):
                banks = [next_pb() for _ in range(4)]
                for sbk in range(4):
                    wt, rw = ring.get()
                    w3 = wt[:, 0:5632].rearrange("p (k c) -> p k c", k=11)
                    for s in range(4):
                        pb, rpb = banks[s]
                        if sbk == 0:
                            pe.wait_for(writes=[rpb])
                        pe.wait_for(reads=[rw, RHID])
                        for kc in range(11):
                            fin = (sbk == 3 and kc == 10)
                            pe.op(lambda: P.matmul(pb[:, :], lhsT=HID[:, sbk * 11 + kc, s * 128:(s + 1) * 128], rhs=w3[:, kc, :],
                                                   start=(sbk == 0 and kc == 0), stop=fin),
                                  reads=[rw, RHID], writes=([rpb] if fin else []), inc=(kc == 10))
                for s in range(4):
                    pb, rpb = banks[s]
                    xr, rxr = XRES[xrrot[0] % 3], RXRES[xrrot[0] % 3]
                    xrrot[0] += 1
                    sp.dma(xr, x1_d[t0 + s * 128:t0 + (s + 1) * 128, fb * 512:(fb + 1) * 512], rxr, writes=[rxr])
                    dve.op(lambda: V.scalar_tensor_tensor(out=PRE[:, s, fb * 512:(fb + 1) * 512], in0=xr, scalar=ALPHA, in1=pb[:, :], op0=ALU.mult, op1=ALU.add),
                           reads=[rxr, rpb], writes=[RPRE[s]])
            for s in range(4):
                layer_norm_rows(PRE[:, s, :], RPRE[s], G2, B2, R_parF, MV, STATS, RSTAT, epst[:, 0:1])
                sp.dma(out_d[t0 + s * 128:t0 + (s + 1) * 128, :], PRE[:, s, :], RPRE[s], reads=[RPRE[s]])
        ring.end()
        fw.barrier()
        print("ninst", {e.name: e.ninst for e in fw.engs}, "nsem", fw.nsem)
    return nc


def _blocks(W, kc, nb):
    return np.ascontiguousarray(W.reshape(kc, 128, nb, 512).transpose(2, 1, 0, 3).reshape(nb, 128, kc * 512))


def prep_shared(inp):
    f = lambda a: np.ascontiguousarray(np.asarray(a, dtype=np.float32))
    w_in = f(inp["w_in"])[0]
    sh = {}
    wq, wk, wv, wu = w_in[:, 0:1024], w_in[:, 1024:2048], w_in[:, 2048:3072], w_in[:, 3072:4096]
    sh["w_in_b"] = np.concatenate([_blocks(np.ascontiguousarray(w), 16, 2) for w in (wq, wk, wu, wv)], axis=0)
    sh["w_glu_b"] = _blocks(f(inp["w_glu"])[0], 8, 2)
    sh["w_out_b"] = _blocks(f(inp["w_out"])[0], 16, 4)
    wfi = f(inp["w_ffn_in"])[0]
    wfi = np.concatenate([wfi[:, :DFF].reshape(D, 22, 256), wfi[:, DFF:].reshape(D, 22, 256)], axis=2).reshape(D, 22 * 512)
    sh["w_ffi_b"] = _blocks(np.ascontiguousarray(wfi), 16, 22)
    wo = f(inp["w_ffn_out"])[0]
    sh["w_ffo_b"] = np.ascontiguousarray(
        wo.reshape(4, 11, 128, 4, 512).transpose(3, 0, 2, 1, 4).reshape(16, 128, 11 * 512))
    sh["ident"] = np.eye(128, dtype=np.float32)
    jj = np.arange(128) // 16
    sh["cmask"] = (jj[None, :] >= jj[:, None]).astype(np.float32)
    kl = np.arange(128)[:, None]
    ql = np.arange(128)[None, :]
    sh["mask0"] = np.where((kl >= 64) & (ql < 64), NEGM, 0.0).astype(np.float32)
    sh["mask4"] = np.where((kl < 64) & (ql >= 64), NEGM, 0.0).astype(np.float32)
    rb = f(inp["attn_rel_bias"])[0]
    idx0 = np.clip(ql - kl, -63, 128) + 63
    idx1 = np.clip(128 + ql - kl, -63, 128) + 63
    sh["bias0"] = np.ascontiguousarray(rb[:, idx0])
    sh["bias1"] = np.ascontiguousarray(rb[:, idx1])
    sh["cfar"] = np.ascontiguousarray(np.broadcast_to(rb[:, 191][None, :], (128, NH)))

    def scan2(a):
        a = a.reshape((2, 32) + a.shape[1:])
        a = np.moveaxis(a, 2, 1)
        return np.ascontiguousarray(a.reshape((128, 32) + a.shape[3:]))

    sh["s_are"] = scan2(f(inp["ssm_a_re"])[0])
    sh["s_aim"] = scan2(f(inp["ssm_a_im"])[0])
    ldt = f(inp["ssm_log_dt"])[0]
    sh["s_ldt"] = scan2(np.ascontiguousarray(np.broadcast_to(ldt[:, None], (64, 64))))
    sh["s_bre"] = scan2(f(inp["ssm_b_re"])[0]).reshape(128, 512)
    sh["s_bim"] = scan2(f(inp["ssm_b_im"])[0]).reshape(128, 512)
    sh["s_cre"] = scan2(np.ascontiguousarray(f(inp["ssm_c_re"])[0].transpose(0, 2, 1))).reshape(128, 512)
    sh["s_cim"] = scan2(np.ascontiguousarray(f(inp["ssm_c_im"])[0].transpose(0, 2, 1))).reshape(128, 512)
    dd = f(inp["ssm_d"])[0]
    sh["s_dcol"] = np.ascontiguousarray(np.tile(dd.T, (8, 1)))
    col = lambda v: np.ascontiguousarray(v.reshape(-1, 128).T)
    sh["b_glu"] = col(f(inp["b_glu"])[0])
    sh["g_att"] = col(f(inp["g_attn_out"])[0])
    sh["g_ssm"] = col(f(inp["g_ssm_out"])[0])
    for k in ("ln1_g", "ln1_b", "ln2_g", "ln2_b"):
        sh[k] = f(inp[k])[0]
    cw = f(inp["ffn_conv_w"])[0]
    sh["conv_w"] = np.ascontiguousarray(cw.reshape(3, 44, 128).transpose(2, 1, 0).reshape(128, 132))
    sh["conv_b"] = col(f(inp["ffn_conv_b"])[0])
    return sh


_NC_CACHE = {}


def kernel(**inputs):
    x = np.asarray(inputs["x"], dtype=np.float32)
    B, L, _ = x.shape
    sh = prep_shared(inputs)
    if L not in _NC_CACHE:
        _NC_CACHE[L] = build(L)
    nc = _NC_CACHE[L]
    in_maps = [dict(sh, x=np.ascontiguousarray(x[b])) for b in range(B)]
    res = run_bass_kernel_spmd(nc, in_maps, core_ids=list(range(B)))
    return np.stack([np.asarray(r["out"], dtype=np.float32) for r in res.results], axis=0)
```
